# Optimizing a Trainium2 kernel written in Bass

```python
import math
import jax, jax.numpy as jnp
from jax import lax
import numpy as np

D_MODEL = 1024
BATCH = 2
SEQ = 8192
DEPTH = 1

CHUNK = 64
D_MIX = D_MODEL
D_SSM = D_MIX // 2
D_ATT = D_MIX - D_SSM
SSM_GROUP = 16
N_SSM_GROUPS = D_SSM // SSM_GROUP
SSM_STATE = 64
HEAD_DIM = 64
N_HEADS = D_ATT // HEAD_DIM
LEFT_CHUNKS = 8
BAND = (LEFT_CHUNKS + 1) * CHUNK
MAX_REL = 128
D_FF = -(-8 * D_MODEL // (3 * 256)) * 256
D_IN = D_SSM + 3 * D_ATT
EPS = 1e-6
DT_MIN = 1e-3
DT_MAX = 1e-1

kernel_name = "hymba_s5_chunked_attention_block"


def rmsnorm(x, g):
    xf = x.astype(jnp.float32)
    y = xf * lax.rsqrt(jnp.mean(xf * xf, axis=-1, keepdims=True) + EPS)
    return (y * g.astype(jnp.float32)).astype(x.dtype)


def s5_mixer(u, lam_re, lam_im, log_dt, b_re, b_im, c_re, c_im, d_skip, w_glu, b_glu):
    bsz, seq, _ = u.shape
    f32 = jnp.float32
    uf = u.astype(f32).reshape(bsz, seq, N_SSM_GROUPS, SSM_GROUP)
    lam = lax.complex(lam_re.astype(f32), lam_im.astype(f32))
    dt = jnp.exp(log_dt.astype(f32))[:, None]
    lam_bar = jnp.exp(lam * dt)
    b_mat = lax.complex(b_re.astype(f32), b_im.astype(f32))
    b_bar = ((lam_bar - 1.0) / lam)[..., None] * b_mat
    bu = jnp.einsum('bsgh,gph->bsgp', uf.astype(jnp.complex64), b_bar)
    a = jnp.broadcast_to(lam_bar, bu.shape)

    def combine(left, right):
        a_l, b_l = left
        a_r, b_r = right
        return a_r * a_l, a_r * b_l + b_r

    _, states = lax.associative_scan(combine, (a, bu), axis=1)
    c_mat = lax.complex(c_re.astype(f32), c_im.astype(f32))
    y = jnp.einsum('bsgp,ghp->bsgh', states, c_mat).real + d_skip.astype(f32) * uf
    y = jax.nn.gelu(y.reshape(bsz, seq, D_SSM))
    y = y * jax.nn.sigmoid(y @ w_glu.astype(f32) + b_glu.astype(f32))
    return y.astype(u.dtype)


def chunked_attention(q, k, v, rel_bias):
    bsz, seq, _ = q.shape
    n_chunks = seq // CHUNK
    shp = (bsz, n_chunks, CHUNK, N_HEADS, HEAD_DIM)
    qc, kc, vc = q.reshape(shp), k.reshape(shp), v.reshape(shp)
    pad = ((0, 0), (LEFT_CHUNKS, 0), (0, 0), (0, 0), (0, 0))
    kp, vp = jnp.pad(kc, pad), jnp.pad(vc, pad)
    kb = jnp.concatenate([kp[:, j:j + n_chunks] for j in range(LEFT_CHUNKS + 1)], axis=2)
    vb = jnp.concatenate([vp[:, j:j + n_chunks] for j in range(LEFT_CHUNKS + 1)], axis=2)
    key_chunk = jnp.arange(n_chunks)[:, None] - LEFT_CHUNKS + (jnp.arange(BAND) // CHUNK)[None, :]
    valid = key_chunk >= 0
    rel = jnp.arange(BAND)[None, :] - LEFT_CHUNKS * CHUNK - jnp.arange(CHUNK)[:, None]
    idx = jnp.clip(rel, -MAX_REL, MAX_REL) + MAX_REL
    bias = rel_bias.astype(jnp.float32)[:, idx]
    scores = jnp.einsum('bnqhd,bnkhd->bhnqk', qc, kb).astype(jnp.float32) / math.sqrt(HEAD_DIM)
    scores = scores + bias[None, :, None, :, :]
    scores = jnp.where(valid[None, None, :, None, :], scores, -1e30)
    probs = jax.nn.softmax(scores, axis=-1)
    out = jnp.einsum('bhnqk,bnkhd->bnqhd', probs.astype(v.dtype), vb)
    return out.reshape(bsz, seq, D_ATT)


def setup_inputs(seed: int = 0) -> dict:
    key = jax.random.key(seed)
    ks = jax.random.split(key, 24)
    f32 = jnp.float32
    L, G, P, H = DEPTH, N_SSM_GROUPS, SSM_STATE, SSM_GROUP

    def nrm(k, shape, scale):
        return jax.random.normal(k, shape, f32) * scale

    def gain(k, n):
        return 1.0 + 0.05 * jax.random.normal(k, (L, n), f32)

    lam_re = -0.5 + 0.01 * jax.random.normal(ks[3], (L, G, P), f32)
    lam_im = math.pi * jnp.arange(P, dtype=f32)[None, None, :] + 0.01 * jax.random.normal(ks[4], (L, G, P), f32)
    log_dt = jax.random.uniform(ks[5], (L, G), f32, math.log(DT_MIN), math.log(DT_MAX))
    return {
        "x": jax.random.normal(ks[0], (BATCH, SEQ, D_MODEL), f32),
        "norm_mix_g": gain(ks[1], D_MODEL),
        "w_in": nrm(ks[2], (L, D_MODEL, D_IN), D_MODEL ** -0.5),
        "ssm_lam_re": lam_re,
        "ssm_lam_im": lam_im,
        "ssm_log_dt": log_dt,
        "ssm_b_re": nrm(ks[6], (L, G, P, H), (2.0 * H) ** -0.5),
        "ssm_b_im": nrm(ks[7], (L, G, P, H), (2.0 * H) ** -0.5),
        "ssm_c_re": nrm(ks[8], (L, G, H, P), (2.0 * P) ** -0.5),
        "ssm_c_im": nrm(ks[9], (L, G, H, P), (2.0 * P) ** -0.5),
        "ssm_d": nrm(ks[10], (L, G, H), 1.0),
        "ssm_w_glu": nrm(ks[11], (L, D_SSM, D_SSM), D_SSM ** -0.5),
        "ssm_b_glu": nrm(ks[12], (L, D_SSM), 0.02),
        "attn_rel_bias": nrm(ks[13], (L, N_HEADS, 2 * MAX_REL + 1), 0.5),
        "norm_ssm_out_g": gain(ks[14], D_SSM),
        "norm_att_out_g": gain(ks[15], D_ATT),
        "w_out": nrm(ks[16], (L, D_MIX, D_MODEL), D_MIX ** -0.5),
        "norm_ffn_g": gain(ks[17], D_MODEL),
        "w_gate": nrm(ks[18], (L, D_MODEL, D_FF), D_MODEL ** -0.5),
        "w_up": nrm(ks[19], (L, D_MODEL, D_FF), D_MODEL ** -0.5),
        "w_down": nrm(ks[20], (L, D_FF, D_MODEL), D_FF ** -0.5),
        "norm_final_g": 1.0 + 0.05 * jax.random.normal(ks[21], (D_MODEL,), f32),
    }


def reference(x, norm_mix_g, w_in, ssm_lam_re, ssm_lam_im, ssm_log_dt, ssm_b_re, ssm_b_im,
              ssm_c_re, ssm_c_im, ssm_d, ssm_w_glu, ssm_b_glu, attn_rel_bias,
              norm_ssm_out_g, norm_att_out_g, w_out, norm_ffn_g, w_gate, w_up, w_down,
              norm_final_g):
    for l in range(DEPTH):
        h = rmsnorm(x, norm_mix_g[l])
        proj = h @ w_in[l]
        u_ssm = proj[..., :D_SSM]
        q = proj[..., D_SSM:D_SSM + D_ATT]
        k = proj[..., D_SSM + D_ATT:D_SSM + 2 * D_ATT]
        v = proj[..., D_SSM + 2 * D_ATT:]
        y_ssm = s5_mixer(u_ssm, ssm_lam_re[l], ssm_lam_im[l], ssm_log_dt[l], ssm_b_re[l], ssm_b_im[l],
                         ssm_c_re[l], ssm_c_im[l], ssm_d[l], ssm_w_glu[l], ssm_b_glu[l])
        y_att = chunked_attention(q, k, v, attn_rel_bias[l])
        mixed = jnp.concatenate([rmsnorm(y_ssm, norm_ssm_out_g[l]),
                                 rmsnorm(y_att, norm_att_out_g[l])], axis=-1)
        x = x + mixed @ w_out[l]
        h2 = rmsnorm(x, norm_ffn_g[l])
        x = x + (jax.nn.silu(h2 @ w_gate[l]) * (h2 @ w_up[l])) @ w_down[l]
    return rmsnorm(x, norm_final_g)
```

```python
import math
from contextlib import ExitStack
import numpy as np
import concourse.bass as bass
import concourse.mybir as mybir
from concourse.bass_utils import run_bass_kernel_spmd

F32 = mybir.dt.float32
BF16 = mybir.dt.bfloat16
I32 = mybir.dt.int32
AF = mybir.ActivationFunctionType
ALU = mybir.AluOpType
AX = mybir.AxisListType

D = 1024
TOK = 2048
WIN = 8192
DFF = 2816
NF = 22
EPS = 1e-6
PI = math.pi


class Rec:
    def __init__(self, nc, stack):
        self.nc = nc
        self.stack = stack
        self.engs = ['pe', 'act', 'dve', 'pool', 'sp']
        self.sem = {}
        self.cnt = {}
        self.lastw = {}
        self.readers = {}
        self.waited = {e: {} for e in self.engs}
        self.ops = {e: [] for e in self.engs}
        self.pending_pe = []
        self.groups = {}

    def seal(self, grp):
        sname = 'd_' + grp
        for tok in self.groups.pop(grp, []):
            tok[1] = self.cnt[sname]

    def get_sem(self, name):
        if name not in self.sem:
            self.sem[name] = self.stack.enter_context(self.nc.semaphore(name))
            self.cnt[name] = 0
        return self.sem[name]

    def op(self, eng, fn, reads=(), writes=(), inc=True, dma=None):
        deps = []
        ps_reads = [k for k in reads if k.startswith('ps')]
        writes = list(writes) + ps_reads
        for k in reads:
            if k in self.lastw:
                deps.append(self.lastw[k])
        for k in writes:
            if k in self.lastw:
                deps.append(self.lastw[k])
            deps.extend(self.readers.get(k, {}).values())
        waits = {}
        for sname, val in deps:
            if eng == 'pe' and sname == 'pe':
                continue
            if val is None and dma is not None and dma.startswith('@') and sname == 'd_' + dma[1:]:
                continue
            assert val is not None, "dependency on unresolved PE op"
            if self.waited[eng].get(sname, 0) >= val:
                continue
            waits[sname] = max(waits.get(sname, 0), val)
        for s, v in waits.items():
            self.waited[eng][s] = v
        if dma is not None and dma.startswith('@'):
            sname = 'd_' + dma[1:]
            self.get_sem(sname)
            self.cnt[sname] += 16
            tok = [sname, None]
            self.groups.setdefault(dma[1:], []).append(tok)
            incspec = (sname, 16)
        elif dma is not None:
            sname = 'd_' + dma
            self.get_sem(sname)
            self.cnt[sname] += 16
            tok = [sname, self.cnt[sname]]
            incspec = (sname, 16)
        elif inc:
            self.get_sem(eng)
            self.cnt[eng] += 1
            tok = [eng, self.cnt[eng]]
            incspec = (eng, 1)
            if eng == 'pe':
                for p in self.pending_pe:
                    p[1] = tok[1]
                self.pending_pe = []
        else:
            assert eng == 'pe'
            tok = ['pe', None]
            self.pending_pe.append(tok)
            incspec = None
        for k in reads:
            self.readers.setdefault(k, {})[(eng, tok[0])] = tok
        for k in writes:
            self.lastw[k] = tok
            self.readers[k] = {}
        self.ops[eng].append((sorted(waits.items()), fn, incspec))

    def emit(self, name):
        assert not self.pending_pe
        with self.nc.Block(name) as blk:
            for e, deco in (('pe', blk.tensor), ('act', blk.scalar), ('dve', blk.vector),
                            ('pool', blk.gpsimd), ('sp', blk.sync)):
                ops = self.ops[e]

                def body(engh, ops=ops):
                    for waits, fn, incspec in ops:
                        for s, v in waits:
                            engh.wait_ge(self.sem[s], v)
                        if fn is None:
                            continue
                        ins = fn(engh)
                        if incspec:
                            ins.then_inc(self.sem[incspec[0]], incspec[1])
                deco(body)
                self.ops[e] = []


def AP(t, off, dims):
    return bass.AP(t.tensor if hasattr(t, 'tensor') else t, off, [list(d) for d in dims])


def build_program():
    nc = bass.Bass("TRN2", target_bir_lowering=False)
    din = {}

    def inp(name, shape):
        din[name] = nc.dram_tensor(name, list(shape), F32, kind="ExternalInput")
        return din[name]

    xw = inp("xw", [WIN, D])
    g_mix = inp("norm_mix_g", [D]); w_in = inp("w_in", [D, 2048])
    lam_re = inp("ssm_lam_re", [32, 64]); lam_im = inp("ssm_lam_im", [32, 64]); log_dt = inp("ssm_log_dt", [32])
    b_re = inp("ssm_b_re", [32, 64, 16]); b_im = inp("ssm_b_im", [32, 64, 16])
    c_re = inp("ssm_c_re", [32, 16, 64]); c_im = inp("ssm_c_im", [32, 16, 64])
    d_sk = inp("ssm_d", [512]); w_glu = inp("ssm_w_glu", [512, 512]); b_glu = inp("ssm_b_glu", [512])
    relb = inp("attn_rel_bias", [8, 257])
    g_ssm = inp("norm_ssm_out_g", [512]); g_att = inp("norm_att_out_g", [512])
    w_out = inp("w_out", [D, D]); g_ffn = inp("norm_ffn_g", [D])
    w_gate = inp("w_gate", [D, DFF]); w_up = inp("w_up", [D, DFF]); w_down = inp("w_down", [DFF, D])
    g_fin = inp("norm_final_g", [D])
    c_ident = inp("c_ident", [128, 128]); c_bdm = inp("c_bdm", [128, 128])
    c_band = inp("c_band", [128, 640]); c_hb = inp("c_hb", [128, 512])
    out = nc.dram_tensor("out", [TOK, D], F32, kind="ExternalOutput")
    wg_s = nc.dram_tensor("wg_s", [D, DFF], BF16, kind="Internal")
    wu_s = nc.dram_tensor("wu_s", [D, DFF], BF16, kind="Internal")
    wd_s = nc.dram_tensor("wd_s", [DFF, D], BF16, kind="Internal")
    relE = nc.dram_tensor("relE", [8, 768], F32, kind="Internal")
    relE2 = nc.dram_tensor("relE2", [8, 128 * 768], F32, kind="Internal")

    st = ExitStack()
    R = Rec(nc, st)

    AF_EL = 12880
    AB_EL = 80640
    arena = {F32: st.enter_context(nc.sbuf_tensor("arena_f", [128, AF_EL], F32)),
             BF16: st.enter_context(nc.sbuf_tensor("arena_b", [128, AB_EL], BF16))}
    free = {F32: [[0, AF_EL]], BF16: [[0, AB_EL]]}

    class Scope:
        def __init__(self):
            self.items = []

        def close(self):
            for dt, off, n in self.items:
                fl = free[dt]
                fl.append([off, off + n])
                fl.sort()
                m = [fl[0]]
                for a, b in fl[1:]:
                    if a <= m[-1][1]:
                        m[-1][1] = max(m[-1][1], b)
                    else:
                        m.append([a, b])
                free[dt] = m
            self.items = []

    pst = Scope()

    def sb(name, shape, dt=F32, stack=None):
        stack = pst if stack is None else stack
        n = int(np.prod(shape[1:]))
        n = (n + 15) // 16 * 16
        for seg in free[dt]:
            if seg[1] - seg[0] >= n:
                off = seg[0]
                seg[0] += n
                break
        else:
            raise RuntimeError("arena full for %s %s %s" % (name, shape, free[dt]))
        stack.items.append((dt, off, n))
        v = arena[dt][:, off:off + int(np.prod(shape[1:]))]
        if len(shape) == 2:
            return v
        names = "abcdefg"[:len(shape) - 1]
        pat = "p (%s) -> p %s" % (" ".join(names), " ".join(names))
        return v.rearrange(pat, **{names[i]: shape[1 + i] for i in range(len(shape) - 2)})

    def dma(eng, o, i, reads, writes, slot):
        R.op(eng, lambda e: e.dma_start(out=o, in_=i, allow_slow_non_contiguous=True), reads, writes, dma=slot)

    def tt(eng, o, a, b, op, r, w):
        R.op(eng, lambda e: e.tensor_tensor(out=o, in0=a, in1=b, op=op), r, w)

    def ts(eng, o, a, s1, s2, op0, op1, r, w):
        if s2 is None:
            R.op(eng, lambda e: e.tensor_scalar(out=o, in0=a, scalar1=s1, scalar2=None, op0=op0), r, w)
        else:
            R.op(eng, lambda e: e.tensor_scalar(out=o, in0=a, scalar1=s1, scalar2=s2, op0=op0, op1=op1), r, w)

    def act(o, i, func, r, w, scale=1.0, bias=None, accum=None):
        kw = {}
        if bias is not None:
            kw['bias'] = bias
        if accum is not None:
            kw['accum_out'] = accum
        R.op('act', lambda e: e.activation(out=o, in_=i, func=func, scale=scale, **kw), r, w)

    def mm(o, lhsT, rhs, start, stop, r, w, tp=None, inc=None):
        kw = {}
        if tp is not None:
            kw['tile_position'] = tp
        R.op('pe', lambda e: e.matmul(o, lhsT, rhs, start=start, stop=stop, **kw), r, w, inc=stop if inc is None else inc)

    def tr(o, i, ident, r, w):
        R.op('pe', lambda e: e.transpose(o, i, ident), r, w)

    def cmul(eng, ore, oim, are, aim, bre, bim, t1, t2, r, w, tk, conj=False):
        tt(eng, t1, are, bre, ALU.mult, r, [tk + '1'])
        tt(eng, t2, aim, bim, ALU.mult, r, [tk + '2'])
        tt(eng, ore, t1, t2, ALU.add if conj else ALU.subtract, [tk + '1', tk + '2'], w)
        tt(eng, t1, aim, bre, ALU.mult, r, [tk + '1'])
        tt(eng, t2, are, bim, ALU.mult, r, [tk + '2'])
        tt(eng, oim, t1, t2, ALU.subtract if conj else ALU.add, [tk + '1', tk + '2'], w)

    ps = st.enter_context(nc.psum_tensor("ps", [128, 8, 512], F32))

    def bank(b):
        return ps[:, b, :]

    ident = sb("ident", [128, 128]); bdm = sb("bdm", [128, 128])
    vecs = sb("vecs", [128, 32])
    gm = vecs[:, 0:8]; gf = vecs[:, 8:16]; gs = vecs[:, 16:20]; ga = vecs[:, 20:24]
    bg = vecs[:, 24:28]; dk = vecs[:, 28:32]
    carry = sb("carry", [128, 2, 16]); send = sb("send", [128, 2, 16])
    r16 = sb("r16", [128, 16])
    ones = sb("ones", [128, 128])
    epsc = sb("epsc", [128, 1])

    dma('sp', ident[:], c_ident.ap(), [], ['ident'], '@cst')
    dma('sp', bdm[:], c_bdm.ap(), [], ['bdm'], '@cst')
    R.seal('cst')
    R.op('pool', lambda e: e.memset(ones[:], 1.0), [], ['ones'])
    R.op('pool', lambda e: e.memset(carry[:], 0.0), [], ['carry'])
    R.op('pool', lambda e: e.memset(epsc[:], EPS), [], ['epsc'])

    wst = Scope()
    winb = sb("winb", [128, 8, 2048], BF16, wst)
    wz = sb("wz", [128, 4, 16, 2, 128], BF16, wst)
    u16c = sb("u16c", [128, 16]); u16s = sb("u16s", [128, 16])
    wsp = nc.dram_tensor("wsp", [128, 16384], BF16, kind="Internal")
    p0 = Scope()
    p0t = Scope()
    tabS = nc.dram_tensor("tabS", [128, 1824], F32, kind="Internal")
    TABS = (("Bb", [128, 2, 16, 16], 0, 512), ("CT", [128, 2, 16, 16], 512, 512), ("CTim", [128, 16, 16], 1024, 256),
            ("Pre", [128, 17, 16], 1280, 272), ("Pim", [128, 17, 16], 1552, 272))
    Bb = sb("Bb", [128, 2, 16, 16], F32, p0)
    CT = sb("CT", [128, 2, 16, 16], F32, p0)
    CTim = sb("CTim", [128, 16, 16], F32, p0)
    Pre = sb("Pre", [128, 17, 16], F32, p0); Pim = sb("Pim", [128, 17, 16], F32, p0)
    lre = sb("lre", [128, 16], F32, p0); lim = sb("lim", [128, 16], F32, p0); ldt = sb("ldt", [128, 16], F32, p0)
    Bsp = sb("Bsp", [128, 2, 16, 16], F32, p0t)
    Cn = sb("Cn", [128, 2, 4, 2, 64], F32, p0t)
    lrow = sb("lrow", [128, 2, 2, 64], F32, p0t)[0:32]
    ldc = sb("ldc", [128, 1], F32, p0t)[0:32]
    dgl = sb("dgl", [128, 32], F32, p0t)[0:32]
    vrow = sb("vrow", [128, 128], F32, p0t)[0:32]
    dma('sp', lrow[:, 0, :, :], AP(lam_re, 0, [[64, 32], [0, 2], [1, 64]]), [], ['lrow'], '@ssmin')
    dma('sp', lrow[:, 1, :, :], AP(lam_im, 0, [[64, 32], [0, 2], [1, 64]]), [], ['lrow'], '@ssmin')
    dma('sp', ldc[:], AP(log_dt, 0, [[1, 32], [1, 1]]), [], ['ldc'], '@ssmin')
    for g2 in range(2):
        hs = slice(64 * g2, 64 * g2 + 64)
        dma('sp', Bsp[hs, 0, :, :], AP(b_re, g2 * 1024, [[16, 64], [2048, 16], [1, 16]]), [], ['Bsp'], '@ssmin')
        dma('sp', Bsp[hs, 1, :, :], AP(b_im, g2 * 1024, [[16, 64], [2048, 16], [1, 16]]), [], ['Bsp'], '@ssmin')
    for ci, csrc in enumerate((c_re, c_im)):
        for ct in range(4):
            dma('sp', Cn[:, ci, ct, :, :], AP(csrc, ct * 8192, [[64, 128], [0, 2], [1, 64]]), [], ['Cn'],
                '@ssmin')
    R.seal('ssmin')
    psrow0 = ps[:].ap[0][0]
    for c_, dst_, key_ in ((0, lre, 'lre'), (1, lim, 'lim')):
        tr(bank(2 + c_)[:, 0:32], lrow[:, c_, :, :].rearrange("p a b -> p (a b)"), ident[0:32, 0:32], ['lrow', 'ident'], ['psb%d' % (2 + c_)])
        for g2 in range(2):
            hs = slice(64 * g2, 64 * g2 + 64)
            src_ = AP(ps, ps[hs, 2 + c_, g2:g2 + 1].offset, [[psrow0, 64], [2, 16]])
            R.op('dve', lambda e, src_=src_, hs=hs, dst_=dst_: e.tensor_copy(out=dst_[hs, :], in_=src_), ['psb%d' % (2 + c_)], [key_])
    ts('dve', dgl[:], ident[0:32, 0:32], ldc[:, 0:1], None, ALU.mult, None, ['ident', 'ldc'], ['dgl'])
    mm(bank(4)[:, 0:32], ones[0:32, :], dgl[:], True, True, ['ones', 'dgl'], ['psb4'])
    for g2 in range(2):
        hs = slice(64 * g2, 64 * g2 + 64)
        src_ = AP(ps, ps[hs, 4, g2:g2 + 1].offset, [[psrow0, 64], [2, 16]])
        R.op('dve', lambda e, src_=src_, hs=hs: e.tensor_copy(out=ldt[hs, :], in_=src_), ['psb4'], ['ldt'])
    r0 = 0
    for (src, n) in ((g_mix, 8), (g_ffn, 8), (g_ssm, 4), (g_att, 4), (b_glu, 4), (d_sk, 4)):
        dma('sp', vrow[r0:r0 + n, :], AP(src, 0, [[128, n], [1, 128]]), [], ['vrow'], '@cst2')
        r0 += n
    R.seal('cst2')
    tr(bank(5)[:, 0:32], vrow[:], ident[0:32, 0:32], ['vrow', 'ident'], ['psb5'])
    R.op('dve', lambda e: e.tensor_copy(out=vecs[:], in_=bank(5)[:, 0:32]), ['psb5'], ['gm', 'gf', 'gs', 'ga', 'bg', 'dk'])
    wstg = Scope()
    stg = [sb("stg%d" % i, [128, 4096], BF16, wstg) for i in range(2)]
    w_in_b = w_in.ap().bitcast(BF16)
    for k in range(8):
        sl_ = k % 2
        dma('sp', stg[sl_][:], w_in_b[k * 128:(k + 1) * 128, :], [], ['stg%d' % sl_], 'stg%d' % sl_)
        act(winb[:, k, :], stg[sl_][:].bitcast(F32), AF.Copy, ['stg%d' % sl_, 'gm'], ['winb%d' % k], scale=gm[:, k:k + 1])
    for ci in range(2):
        for ct in range(4):
            tr(bank(ct % 2)[:, 0:128], Cn[:, ci, ct, :, :].rearrange("p a b -> p (a b)"), ident[:],
               ['Cn', 'ident'], ['psb%d' % (ct % 2)])
            for g2 in range(2):
                hs = slice(64 * g2, 64 * g2 + 64)
                src = AP(ps, ps[hs, ct % 2, g2 * 16:g2 * 16 + 1].offset, [[ps[:].ap[0][0], 64], [32, 4], [1, 16]])
                if ci == 0:
                    R.op('dve', lambda e, s=src, hs=hs, ct=ct: e.tensor_copy(out=CT[hs, 0, 4 * ct:4 * ct + 4, :], in_=s),
                         ['psb%d' % (ct % 2)], ['CT'])
                else:
                    R.op('dve', lambda e, s=src, hs=hs, ct=ct: e.tensor_copy(out=CTim[hs, 4 * ct:4 * ct + 4, :], in_=s),
                         ['psb%d' % (ct % 2)], ['CTim'])
    ts('dve', CT[:, 1, :, :], CTim[:], -1.0, None, ALU.mult, None, ['CTim'], ['CT'])

    dt_ = sb("dt_", [128, 16], F32, p0); aa = sb("aa", [128, 16], F32, p0); th = sb("th", [128, 16], F32, p0)
    tA = sb("tA", [128, 16], F32, p0); tB = sb("tB", [128, 16], F32, p0)
    cs1 = sb("cs1", [128, 16], F32, p0); sn1 = sb("sn1", [128, 16], F32, p0)
    act(dt_[:], ldt[:], AF.Exp, ['ldt'], ['dt_'])
    tt('dve', aa[:], lre[:], dt_[:], ALU.mult, ['lre', 'dt_'], ['aa'])
    tt('dve', th[:], lim[:], dt_[:], ALU.mult, ['lim', 'dt_'], ['th'])

    act(sn1[:], th[:], AF.Sin, ['th'], ['sn1'], scale=1.0 / 16)
    ts('dve', tA[:], th[:], 1.0 / 16, PI / 2, ALU.mult, ALU.add, ['th'], ['tA'])
    act(cs1[:], tA[:], AF.Sin, ['tA'], ['cs1'])
    for _ in range(4):
        tt('dve', tA[:], cs1[:], cs1[:], ALU.mult, ['cs1'], ['tA'])
        tt('dve', tB[:], sn1[:], sn1[:], ALU.mult, ['sn1'], ['tB'])
        tt('dve', sn1[:], sn1[:], cs1[:], ALU.mult, ['sn1', 'cs1'], ['sn1'])
        ts('dve', sn1[:], sn1[:], 2.0, None, ALU.mult, None, ['sn1'], ['sn1'])
        tt('dve', cs1[:], tA[:], tB[:], ALU.subtract, ['tA', 'tB'], ['cs1'])

    ut_c = sb("ut_c", [128, 17, 16], F32, p0); ut_s = sb("ut_s", [128, 17, 16], F32, p0)
    tmp1 = sb("tmp1", [128, 1024], F32, p0t); tmp2 = sb("tmp2", [128, 1024], F32, p0t)

    def powtab_steps(tc, tsn, bc, bs, N, key, rk, tmp1, tmp2):
        steps = []

        def init():
            R.op('dve', lambda e: e.memset(tc[:, 0, :], 1.0), [], [key])
            R.op('dve', lambda e: e.memset(tsn[:, 0, :], 0.0), [], [key])
            R.op('dve', lambda e: e.tensor_copy(out=tc[:, 1, :], in_=bc), rk, [key])
            R.op('dve', lambda e: e.tensor_copy(out=tsn[:, 1, :], in_=bs), rk, [key])
        steps.append(init)
        k = 1
        while k < N:
            n = min(k, N - k)

            def step_re(k=k, n=n):
                bcst = lambda t: AP(t, t[:, k, :].offset, [[t[:].ap[0][0], 128], [0, n], [1, 16]])
                t1 = tmp1[:, 0:n * 16].rearrange("p (a b) -> p a b", b=16)
                t2 = tmp2[:, 0:n * 16].rearrange("p (a b) -> p a b", b=16)
                tt('dve', t1, tc[:, 1:1 + n, :], bcst(tc), ALU.mult, [key], ['tmp1'])
                tt('dve', t2, tsn[:, 1:1 + n, :], bcst(tsn), ALU.mult, [key], ['tmp2'])
                tt('dve', tc[:, k + 1:k + 1 + n, :], t1, t2, ALU.subtract, ['tmp1', 'tmp2'], [key])

            def step_im(k=k, n=n):
                bcst = lambda t: AP(t, t[:, k, :].offset, [[t[:].ap[0][0], 128], [0, n], [1, 16]])
                t1 = tmp1[:, 0:n * 16].rearrange("p (a b) -> p a b", b=16)
                t2 = tmp2[:, 0:n * 16].rearrange("p (a b) -> p a b", b=16)
                tt('dve', t1, tsn[:, 1:1 + n, :], bcst(tc), ALU.mult, [key], ['tmp1'])
                tt('dve', t2, tc[:, 1:1 + n, :], bcst(tsn), ALU.mult, [key], ['tmp2'])
                tt('dve', tsn[:, k + 1:k + 1 + n, :], t1, t2, ALU.add, ['tmp1', 'tmp2'], [key])
            steps.append(step_re)
            steps.append(step_im)
            k += n
        return steps

    def powtab(*a):
        for st_ in powtab_steps(*a):
            st_()
    powtab(ut_c, ut_s, cs1[:], sn1[:], 16, 'ut', ['cs1', 'sn1'], tmp1, tmp2)
    R.op('dve', lambda e: e.tensor_copy(out=u16c[:], in_=ut_c[:, 16, :]), ['ut'], ['u16'])
    R.op('dve', lambda e: e.tensor_copy(out=u16s[:], in_=ut_s[:, 16, :]), ['ut'], ['u16'])

    Mg = sb("Mg", [128, 17, 16], F32, p0)
    for n in range(17):
        act(Mg[:, n, :], aa[:], AF.Exp, ['aa'], ['Mg'], scale=float(n))
    tt('dve', Pre[:], Mg[:], ut_c[:], ALU.mult, ['Mg', 'ut'], ['P'])
    tt('dve', Pim[:], Mg[:], ut_s[:], ALU.mult, ['Mg', 'ut'], ['P'])
    act(r16[:], aa[:], AF.Exp, ['aa'], ['r16'], scale=16.0)

    fre = sb("fre", [128, 16], F32, p0); fim = sb("fim", [128, 16], F32, p0)
    nr = sb("nr", [128, 16], F32, p0); den = sb("den", [128, 16], F32, p0)
    ts('dve', nr[:], Pre[:, 1, :], -1.0, None, ALU.add, None, ['P'], ['nr'])
    tt('dve', den[:], lre[:], lre[:], ALU.mult, ['lre'], ['den'])
    tt('dve', tA[:], lim[:], lim[:], ALU.mult, ['lim'], ['tA'])
    tt('dve', den[:], den[:], tA[:], ALU.add, ['den', 'tA'], ['den'])
    R.op('dve', lambda e: e.reciprocal(out=den[:], in_=den[:]), ['den'], ['den'])
    tt('dve', tA[:], nr[:], lre[:], ALU.mult, ['nr', 'lre'], ['tA'])
    tt('dve', tB[:], Pim[:, 1, :], lim[:], ALU.mult, ['P', 'lim'], ['tB'])
    tt('dve', tA[:], tA[:], tB[:], ALU.add, ['tA', 'tB'], ['tA'])
    tt('dve', fre[:], tA[:], den[:], ALU.mult, ['tA', 'den'], ['fre'])
    tt('dve', tA[:], Pim[:, 1, :], lre[:], ALU.mult, ['P', 'lre'], ['tA'])
    tt('dve', tB[:], nr[:], lim[:], ALU.mult, ['nr', 'lim'], ['tB'])
    tt('dve', tA[:], tA[:], tB[:], ALU.subtract, ['tA', 'tB'], ['tA'])
    tt('dve', fim[:], tA[:], den[:], ALU.mult, ['tA', 'den'], ['fim'])

    def b16(t, n=None):
        return AP(t, t[:].offset, [[t[:].ap[0][0], 128], [1, 16], [0, 16]])
    t1v = tmp1[:, 0:256].rearrange("p (a b) -> p a b", b=16)
    t2v = tmp2[:, 0:256].rearrange("p (a b) -> p a b", b=16)
    cmul('dve', Bb[:, 0, :, :], Bb[:, 1, :, :], Bsp[:, 0, :, :], Bsp[:, 1, :, :], b16(fre), b16(fim), t1v, t2v,
         ['Bsp', 'fre', 'fim'], ['Bb'], 'tmp')

    prow = Pre[:].ap[0][0]

    def gen_lag(scope, do_fir, wfir=None, extra_ops=None):
        CTd = sb("CTd", [128, 2, 16, 2, 16], F32, scope)
        E = [sb("E%d" % i, [128, 2, 16, 2, 16], F32, scope) for i in range(3)]
        g1 = sb("g1", [128, 16, 16], F32, scope); g2t = sb("g2t", [128, 16, 16], F32, scope)
        g3 = sb("g3", [128, 128], F32, scope)
        Gtc = [sb("Gt%d" % c_, [128, 16, 16], F32, scope) for c_ in range(2)]
        Eb = [sb("Eb%d" % i, [128, 2, 16, 2, 16], BF16, scope) for i in range(2)]
        identb0 = sb("identb0", [128, 128], BF16, scope)
        R.op('dve', lambda e: e.tensor_copy(out=identb0[:], in_=ident[:]), ['ident'], ['identb0'])
        ptw = [ps[:, 4 + a_, :].bitcast(BF16) for a_ in range(4)]
        if True:
            for g2 in range(2):
                R.op('dve', lambda e, g2=g2: e.tensor_copy(out=CTd[:, :, :, g2, :], in_=CT[:]), ['CT'], ['CTd'])
        for i in range(3):
            R.op('pool', lambda e, i=i: e.memset(E[i][:], 0.0), [], ['E%d' % i])
        def gen_E(tau):
            Ei = E[tau % 3]; ek = 'E%d' % (tau % 3)
            pb = lambda t: AP(t, t[:, tau, :].offset, [[prow, 128], [1, 16], [0, 16]])
            cmul('dve', Gtc[0][:], Gtc[1][:], Bb[:, 0, :, :], Bb[:, 1, :, :], pb(Pre), pb(Pim),
                 g1[:], g2t[:], ['Bb', 'P'], ['Gt'], 'gt')
            for g2 in range(2):
                hs = slice(64 * g2, 64 * g2 + 64)
                for c_ in range(2):
                    R.op('act', lambda e, hs=hs, g2=g2, Ei=Ei, c_=c_: e.copy(out=Ei[hs, c_, :, g2, :], in_=Gtc[c_][hs, :, :]), ['Gt'], [ek])
        gen_E(0)
        for tau in range(16):
            Ei = E[tau % 3]; ek = 'E%d' % (tau % 3)
            if tau + 1 < 16:
                gen_E(tau + 1)
            if extra_ops:
                extra_ops.pop(0)()
            for ct in range(4):
                if True:
                    bk = 2 + (ct % 2)
                    for c in range(2):
                        mm(bank(bk)[:, 0:128], Ei[:, c, 4 * ct:4 * ct + 4, :, :].rearrange("p a b c -> p (a b c)"),
                           CTd[:, c, 4 * ct:4 * ct + 4, :, :].rearrange("p a b c -> p (a b c)"), c == 0, c == 1,
                           [ek, 'CTd'], ['psb%d' % bk])
                    if tau == 0:
                        tt('dve', g3[:], bank(bk)[:, 0:128], bdm[:], ALU.mult, ['psb%d' % bk, 'bdm'], ['g3'])
                        R.op('dve', lambda e, ct=ct: e.scalar_tensor_tensor(out=wfir[:, 0, ct, :], in0=ident[:], scalar=dk[:, ct:ct + 1],
                                                                           in1=g3[:], op0=ALU.mult, op1=ALU.add),
                             ['g3', 'ident', 'dk'], ['wfir'])
                    else:
                        tt('dve', wfir[:, tau, ct, :], bank(bk)[:, 0:128], bdm[:], ALU.mult, ['psb%d' % bk, 'bdm'], ['wfir'])
                if True:
                    if ct == 0:
                        R.op('act', lambda e, Ei=Ei, tau=tau: e.copy(out=Eb[tau % 2][:], in_=Ei[:]), [ek], ['Eb%d' % (tau % 2)])
                    for c in range(2):
                        bk2 = 4 + ((ct * 2 + c) % 4)
                        R.op('pe', lambda e, bk2=bk2, ct=ct, c=c, tau=tau: e.transpose(
                            ptw[bk2 - 4][:, 0:128], Eb[tau % 2][:, c, 4 * ct:4 * ct + 4, :, :].rearrange("p a b c -> p (a b c)"), identb0[:]),
                            ['Eb%d' % (tau % 2), 'identb0'], ['psb%d' % bk2])
                        R.op('act', lambda e, bk2=bk2, ct=ct, c=c, j=15 - tau: e.copy(out=wz[:, ct, j, c, :], in_=ptw[bk2 - 4][:, 0:128]),
                             ['psb%d' % bk2], ['wz'])
    R.emit("p0a")
    p0t.close()
    wstg.close()
    wfir = sb("wfir", [128, 16, 4, 128], BF16, p0)
    wy = sb("wy", [128, 2, 16, 256], BF16, p0)
    def pbi(t, ih):
        return AP(t, t[:, 1 + 8 * ih, :].offset, [[prow, 128], [1, 16], [16, 8], [0, 16]])

    def cb(t, c=None):
        base = t[:, c, :, :] if c is not None else t[:]
        return AP(t, base.offset, [[base.ap[0][0], 128], [16, 16], [0, 8], [1, 16]])
    big1 = sb("big1", [128, 2048], F32, p0); big2 = sb("big2", [128, 2048], F32, p0)
    b1v = big1[:].rearrange("p (a b c) -> p a b c", a=16, b=8)
    b2v = big2[:].rearrange("p (a b c) -> p a b c", a=16, b=8)
    wy_ops = []
    for ih in range(2):
        w0 = wy[:, 0, :, :].rearrange("p a (b c) -> p a b c", b=16)[:, :, 8 * ih:8 * ih + 8, :]
        w1 = wy[:, 1, :, :].rearrange("p a (b c) -> p a b c", b=16)[:, :, 8 * ih:8 * ih + 8, :]
        wy_ops.append(lambda ih=ih: tt('dve', b1v, cb(CT, 0), pbi(Pre, ih), ALU.mult, ['CT', 'P'], ['big1']))
        wy_ops.append(lambda ih=ih: tt('dve', b2v, cb(CTim), pbi(Pim, ih), ALU.mult, ['CTim', 'P'], ['big2']))
        wy_ops.append(lambda w0=w0: tt('dve', w0, b1v, b2v, ALU.subtract, ['big1', 'big2'], ['wy']))
        wy_ops.append(lambda ih=ih: tt('dve', b1v, cb(CT, 0), pbi(Pim, ih), ALU.mult, ['CT', 'P'], ['big1']))
        wy_ops.append(lambda ih=ih: tt('dve', b2v, cb(CTim), pbi(Pre, ih), ALU.mult, ['CTim', 'P'], ['big2']))
        wy_ops.append(lambda: tt('dve', b1v, b1v, b2v, ALU.add, ['big1', 'big2'], ['big1']))
        wy_ops.append(lambda w1=w1: ts('dve', w1, b1v, -1.0, None, ALU.mult, None, ['big1'], ['wy']))
    gen_lag(p0, True, wfir, wy_ops)
    dma('pool', relE.ap()[:, 511:767], relb.ap()[:, 0:256], [], ['relEa'], '@relE')
    dma('pool', AP(relE, 0, [[768, 8], [1, 511], [1, 1]]), AP(relb, 0, [[257, 8], [0, 511], [1, 1]]), [], ['relEb'], '@relE')
    dma('pool', relE.ap()[:, 767:768], relb.ap()[:, 0:1], [], ['relEc'], '@relE')
    R.seal('relE')
    for hd in range(8):
        dma('pool', AP(relE2, hd * 98304, [[768, 128], [1, 768]]), AP(relE, hd * 768, [[0, 128], [1, 768]]),
            ['relEa', 'relEb', 'relEc'], ['relS%d' % hd], '@relS')
    R.seal('relS')
    while wy_ops:
        wy_ops.pop(0)()
    dma('sp', wsp.ap()[:, 0:8192], wfir[:].rearrange("p a b c -> p (a b c)"), ['wfir'], ['wsp'], '@wsp')
    dma('sp', wsp.ap()[:, 8192:16384], wy[:].rearrange("p a b c -> p (a b c)"), ['wy'], ['wsp'], '@wsp')
    R.seal('wsp')
    R.op('sp', None, ['wsp'], [])
    R.emit("p0")
    p0.close()

    ucb = sb("ucb", [128, 129, 16], F32, wst); usb = sb("usb", [128, 129, 16], F32, wst)
    ust = Scope()
    sprev = sb("sprev", [128, 2, 16, 128], BF16, ust)
    uT = sb("uT", [128, 4, TOK], BF16, ust)
    p2 = Scope()
    qT = sb("qT", [128, 4, TOK], BF16, p2)
    kT = sb("kT", [128, 4, TOK + 512], BF16, p2)
    Vt = sb("Vt", [128, 20, 512], BF16, p2)
    p1 = Scope()
    NXT = 3
    NDG = 3
    NPG = 4
    xt = [sb("xt%d" % i, [128, D], F32, p1) for i in range(NXT)]
    xb = [sb("xb%d" % i, [128, D], BF16, p1) for i in range(2)]
    identb1 = sb("identb1", [128, 128], BF16, p1)
    hT = sb("hT", [128, 8, 512], BF16, p1)
    ssq = sb("ssq", [128, NDG], F32, p1)
    Zbs = [sb("Zb%d" % i, [128, 2, NPG, 128], F32, p1) for i in range(2)]
    Zm = sb("Zm", [128, 2, NPG, 128], F32, p1)
    ztj = sb("ztj", [128, 2 * NPG * 128], F32, p1)
    zt1 = ztj[:, 0:NPG * 128].rearrange("p (a b) -> p a b", a=NPG)
    zt2 = ztj[:, NPG * 128:2 * NPG * 128].rearrange("p (a b) -> p a b", a=NPG)
    junk = ztj
    D0g = sb("D0g", [128, NPG, 128], F32, p1)
    cf = sb("cf", [128, 2, NPG], F32, p1)
    R.op('dve', lambda e: e.tensor_copy(out=identb1[:], in_=ident[:]), ['ident'], ['identb1'])
    ub_steps = powtab_steps(ucb, usb, u16c[:], u16s[:], 128, 'ub', ['u16'],
                            Zbs[0][:].rearrange("p a b c -> p (a b c)"), Zbs[1][:].rearrange("p a b c -> p (a b c)"))
    ptx = [ps[:, a_, :].bitcast(BF16) for a_ in range(4)]

    def tabv(t, lo, n, pg):
        return AP(t, t[:, lo, NPG * pg:NPG * pg + 1].offset, [[t[:].ap[0][0], 128], [1, NPG], [16, n]])


    def stageL(gt):
        xs = gt % NXT
        dma('sp', xt[xs][:], xw.ap()[gt * 128:(gt + 1) * 128, :], [], ['xt%d' % xs], 'xt%d' % xs)

    def stageA(gt):
        xs = gt % NXT
        s_ = gt % NDG
        b_ = gt % 2
        xk = 'xt%d' % xs; sk = 'ssq%d' % s_
        R.op('dve', lambda e, xs=xs, s_=s_: e.scalar_tensor_tensor(out=junk[:], in0=xt[xs][:], scalar=1.0, in1=xt[xs][:],
                                                                   op0=ALU.mult, op1=ALU.mult, accum_out=ssq[:, s_:s_ + 1]),
             [xk], ['zt1', 'zt2', sk])
        act(ssq[:, s_:s_ + 1], ssq[:, s_:s_ + 1], AF.Sqrt, [sk, 'epsc'], [sk], scale=1.0 / D, bias=epsc[:])
        R.op('dve', lambda e, o_=ssq[:, s_:s_ + 1]: e.reciprocal(out=o_, in_=o_), [sk], [sk])
        act(xb[b_][:], xt[xs][:], AF.Copy, [xk, sk], ['xb%d' % b_], scale=ssq[:, s_:s_ + 1])

    def stageB(gt, tl, ch):
        b_ = gt % 2
        bk = gt % 4
        for k in range(8):
            R.op('pe', lambda e, k=k, bk=bk, b_=b_: e.transpose(ptx[bk][:, k * 128:(k + 1) * 128], xb[b_][:, k * 128:(k + 1) * 128], identb1[:]),
                 ['xb%d' % b_, 'identb1'], ['psb%d' % bk], inc=(k == 7))
        src = ptx[bk].rearrange("p (a b) -> p a b", a=8)
        dst = hTb[ch % 2][:, :, tl * 128:(tl + 1) * 128]
        R.op('dve', lambda e, src=src, dst=dst: e.tensor_copy(out=dst, in_=src), ['psb%d' % bk], ['hT%d' % (ch % 2)])

    hTb = [hT, sprev[:].rearrange("p a b c -> p (a b c)").rearrange("p (a b) -> p a b", a=8)]

    def projpart(ch, part):
        hb_ = hTb[ch % 2]; hk = 'hT%d' % (ch % 2)
        wcol = ch % 4
        ct = part
        bk = 4 + ct % 2
        for k in range(8):
            mm(bank(bk), winb[:, k, ct * 128:(ct + 1) * 128], hb_[:, k, :], k == 0, k == 7, ['winb%d' % k, hk], ['psb%d' % bk])
        R.op('act', (lambda e, ct=ct, bk=bk, wcol=wcol: e.copy(
            out=uT[:, ct, wcol * 512:(wcol + 1) * 512], in_=bank(bk))), ['psb%d' % bk], ['uT'])
        if ch >= 12:
            oc = ch - 12
            bk = 6
            for k in range(8):
                mm(bank(bk), winb[:, k, 512 + ct * 128:512 + (ct + 1) * 128], hb_[:, k, :], k == 0, k == 7,
                   ['winb%d' % k, hk], ['psb%d' % bk])
            R.op('dve', (lambda e, ct=ct, bk=bk, oc=oc: e.tensor_copy(
                out=qT[:, ct, oc * 512:(oc + 1) * 512], in_=bank(bk))), ['psb%d' % bk], ['qT'])
        if ch >= 11:
            kc = ch - 11
            bk = 7
            for k in range(8):
                mm(bank(bk), winb[:, k, 1024 + ct * 128:1024 + (ct + 1) * 128], hb_[:, k, :], k == 0, k == 7,
                   ['winb%d' % k, hk], ['psb%d' % bk])
            R.op('act', (lambda e, ct=ct, bk=bk, kc=kc: e.copy(
                out=kT[:, ct, kc * 512:(kc + 1) * 512], in_=bank(bk))), ['psb%d' % bk], ['kT'])
            tl = part
            bk = 6 if ch == 11 else 5
            for k in range(8):
                mm(bank(bk), hb_[:, k, tl * 128:(tl + 1) * 128], winb[:, k, 1536:2048], k == 0, k == 7,
                   ['winb%d' % k, hk], ['psb%d' % bk])
            R.op('dve', (lambda e, tl=tl, bk=bk, kc=kc: e.tensor_copy(
                out=Vt[:, kc * 4 + tl, :], in_=bank(bk))), ['psb%d' % bk], ['Vt'])

    def zscan(ch, own):
        for pg in range(16 // NPG):
            psl = slice(NPG * pg, NPG * pg + NPG)
            Zb = Zbs[pg % 2]; zk = 'Zb%d' % (pg % 2)
            for j in range(16):
                for pl in range(NPG):
                    pair = NPG * pg + pl
                    ct, kk = pair // 4, pair % 4
                    for c in range(2):
                        bk = 2 * pl + c
                        rhs = AP(uT, uT[32 * kk:32 * kk + 32, ct, j:j + 1].offset, [[uT[:].ap[0][0], 32], [16, 128]])
                        mm(bank(bk)[:, 0:128], wz[32 * kk:32 * kk + 32, ct, j, c, :], rhs, j == 0, j == 15,
                           ['wz', 'uT'], ['psb%d' % bk], tp=(32 * kk, 0))
            for pl in range(NPG):
                for c in range(2):
                    bk = 2 * pl + c
                    R.op('act', (lambda e, pl=pl, c=c, bk=bk, Zb=Zb: e.copy(out=Zb[:, c, pl, :], in_=bank(bk)[:, 0:128])),
                         ['psb%d' % bk], [zk])
            cb_, sb_ = tabv(ucb, 0, 128, pg), tabv(usb, 0, 128, pg)
            tt('dve', zt1, Zb[:, 1, :, :], sb_, ALU.mult, [zk, 'ub'], ['zt1'])
            tt('dve', Zm[:, 0, :, :], Zb[:, 0, :, :], cb_, ALU.mult, [zk, 'ub'], ['Zm0'])
            tt('dve', Zm[:, 0, :, :], Zm[:, 0, :, :], zt1, ALU.add, ['Zm0', 'zt1'], ['Zm0'])
            tt('pool', zt2, Zb[:, 0, :, :], sb_, ALU.mult, [zk, 'ub'], ['zt2'])
            tt('pool', Zm[:, 1, :, :], Zb[:, 1, :, :], cb_, ALU.mult, [zk, 'ub'], ['Zm1'])
            tt('pool', Zm[:, 1, :, :], Zm[:, 1, :, :], zt2, ALU.subtract, ['Zm1', 'zt2'], ['Zm1'])
            r_b = AP(r16, r16[:, NPG * pg:NPG * pg + 1].offset, [[r16[:].ap[0][0], 128], [1, NPG], [0, 128]])
            R.op('dve', lambda e, r_b=r_b: e.tensor_copy(out=D0g[:], in_=r_b), ['r16'], ['D0g'])
            R.op('dve', lambda e: e.memset(D0g[:, :, 0], 0.0), [], ['D0g'])
            r_c = AP(r16, r16[:, NPG * pg:NPG * pg + 1].offset, [[r16[:].ap[0][0], 128], [0, 2], [1, NPG]])
            tt('dve', cf[:], carry[:, :, psl], r_c, ALU.mult, ['carry', 'r16'], ['cf'])
            for c in range(2):
                tt('dve', Zm[:, c, :, 0], Zm[:, c, :, 0], cf[:, c, :], ALU.add, ['Zm%d' % c, 'cf'], ['Zm%d' % c])
                R.op('dve', lambda e, c=c, Zb=Zb: e.tensor_tensor_scan(
                    out=Zb[:, c, :, :].rearrange("p a b -> p (a b)"), data0=D0g[:].rearrange("p a b -> p (a b)"),
                    data1=Zm[:, c, :, :].rearrange("p a b -> p (a b)"), initial=0.0,
                    op0=ALU.mult, op1=ALU.add), ['Zm%d' % c, 'D0g'], [zk])
            if own:
                R.op('act', lambda e, psl=psl: e.copy(out=sprev[:, :, psl, 0], in_=send[:, :, psl]), ['send'], ['sprev', 'hT1'])
                cmul('dve', Zm[:, 0, :, 1:128], Zm[:, 1, :, 1:128], Zb[:, 0, :, 0:127], Zb[:, 1, :, 0:127],
                     tabv(ucb, 0, 127, pg), tabv(usb, 0, 127, pg), zt1[:, :, 0:127], zt2[:, :, 0:127], [zk, 'ub'], ['Zm0', 'Zm1'], 'zt')
                R.op('act', lambda e, psl=psl: e.copy(out=sprev[:, :, psl, 1:128], in_=Zm[:, :, :, 1:128]), ['Zm0', 'Zm1'], ['sprev', 'hT1'])
            else:
                l1 = zt1[:, :, 0]; l2 = zt2[:, :, 0]
                if ch == 11:
                    cmul('dve', send[:, 0, psl], send[:, 1, psl], Zb[:, 0, :, 127], Zb[:, 1, :, 127], ucb[:, 127, psl], usb[:, 127, psl],
                         l1, l2, [zk, 'ub'], ['send'], 'zt')
                cmul('dve', carry[:, 0, psl], carry[:, 1, psl], Zb[:, 0, :, 127], Zb[:, 1, :, 127], ucb[:, 128, psl], usb[:, 128, psl],
                     l1, l2, [zk, 'ub'], ['carry'], 'zt')

    for g_ in range(2):
        stageL(g_)
    stageA(0)
    for ch in range(16):
        for tl in range(4):
            gt = ch * 4 + tl
            if gt + 2 < 64:
                stageL(gt + 2)
            if gt + 1 < 64:
                stageA(gt + 1)
            stageB(gt, tl, ch)
            if ub_steps:
                ub_steps.pop(0)()
            if ch >= 1:
                projpart(ch - 1, tl)
        if ch >= 1 and (ch - 1) % 4 == 3:
            zscan(ch - 1, ch - 1 >= 12)
    for part in range(4):
        projpart(15, part)
    zscan(15, True)
    R.emit("p1")
    p1.close()
    wst.close()

    mixT = sb("mixT", [128, 8, TOK], BF16)
    w4 = Scope()
    wfir = sb("wfir", [128, 16, 4, 128], BF16, w4)
    p3 = Scope()
    Tb = sb("Tb", [128, 8, 640], F32, p3)
    b8 = sb("b8", [128, 640], F32, p3)
    Thi = sb("Thi", [128, 8, 640], BF16, p3)
    hbt = sb("hbt", [128, 512], F32, p3)
    hb8 = sb("hb8", [128, 512], BF16, p3)
    bandt = sb("bandt", [128, 640], F32, p3)
    qzt = [sb("qzt%d" % i, [128, 2, 4, 128], BF16, p3) for i in range(2)]
    Pf = [sb("Pf%d" % i, [128, 640], BF16, p3) for i in range(2)]
    PT = [sb("PT%d" % i, [128, 5, 128], BF16, p3) for i in range(2)]
    identb = sb("identb", [128, 128], BF16, p3)
    Oat = [sb("Oat%d" % i, [128, 512], F32, p3) for i in range(2)]
    On = [sb("On%d" % i, [128, 512], F32, p3) for i in range(2)]
    stat = sb("stat", [128, 8, 4], F32, p3)
    ast = sb("ast", [128, 2], F32, p3)
    R.op('dve', lambda e: e.tensor_copy(out=identb[:], in_=ident[:]), ['ident'], ['identb'])
    for b_ in range(2):
        R.op('pool', lambda e, b_=b_: e.memset(qzt[b_][:], 0.0), [], ['qzt%d' % b_])
    dma('sp', hbt[:], c_hb.ap(), [], ['hbt'], 'hbt')
    dma('sp', bandt[:], c_band.ap(), [], ['bandt'], 'bandt')
    ts('dve', hb8[:], hbt[:], 1.0, None, ALU.mult, None, ['hbt'], ['hb8'])
    for hd in range(8):
        dma('sp', Tb[:, hd, :], AP(relE2, hd * 98304 + 127, [[767, 128], [1, 640]]), ['relS%d' % hd], ['Tb%d' % hd, 'TbL%d' % hd], '@Tb')
    R.seal('Tb')
    dma('sp', wfir[:].rearrange("p a b c -> p (a b c)"), wsp.ap()[:, 0:8192], ['wsp'] + ['TbL%d' % q_ for q_ in range(8)], ['wfir'], 'wfirL')
    def tprep(hd):
        R.op('dve', lambda e, hd=hd: e.scalar_tensor_tensor(out=Tb[:, hd, :], in0=Tb[:, hd, :], scalar=1.0, in1=b8[:], op0=ALU.mult, op1=ALU.add),
             ['Tb%d' % hd, 'b8'], ['Tb%d' % hd])
        R.op('dve', lambda e, hd=hd: e.tensor_copy(out=Thi[:, hd, :], in_=Tb[:, hd, :]), ['Tb%d' % hd], ['Thi%d' % hd])
    ts('dve', b8[:], bandt[:], 1.0, None, ALU.mult, None, ['bandt'], ['b8'])
    for ct_ in range(4):
        ts('dve', qT[:, ct_, :], qT[:, ct_, :], 0.125, None, ALU.mult, None, ['qT'], ['qT'])
    for (src, dst, key) in ((w_gate, wg_s, 'wg_s'), (w_up, wu_s, 'wu_s')):
        for h in range(2):
            dma('pool', AP(dst, h * 512 * DFF, [[DFF, 512], [1408, 2], [1, 1408]]),
                AP(src, h * 512 * DFF, [[DFF, 512], [1408, 2], [1, 1408]]), ['TbL%d' % q_ for q_ in range(8)], [key + str(h)], '@ffnw')
    for h in range(2):
        dma('pool', wd_s.ap()[h * 1408:(h + 1) * 1408, :], w_down.ap()[h * 1408:(h + 1) * 1408, :], ['TbL%d' % q_ for q_ in range(8)],
            ['wd_s%d' % h], '@ffnw')
    R.seal('ffnw')
    psrow = ps[:].ap[0][0]
    ptb = [ps[:, 4 + a_, :].bitcast(BF16) for a_ in range(2)]

    def S1(n):
        i, hd = n // 8, n % 8
        tq, h2 = hd // 2, hd % 2
        a_ = n % 2
        sbk = 'pss%d' % a_
        qb = i % 2
        if n < 8:
            tprep(hd)
        if hd == 0:
            for i2 in ([0, 1] if i == 0 else ([i + 1] if i + 1 < 16 else [])):
                for hh in range(2):
                    hs = slice(64 * hh, 64 * hh + 64)
                    R.op('act' if hh else 'pool', (lambda e, hh=hh, hs=hs, i2=i2: (e.copy if hasattr(e, 'copy') else e.tensor_copy)(
                        out=qzt[i2 % 2][hs, hh, :, :], in_=qT[hs, :, 128 * i2:128 * i2 + 128])), ['qT'], ['qzt%d' % (i2 % 2)])
        for (c0_, nn, bk) in ((0, 512, 2 * a_), (512, 128, 2 * a_ + 1)):
            o_ = bank(bk)[:, 0:nn]
            mm(o_, qzt[qb][:, h2, tq, :], kT[:, tq, 128 * i + c0_:128 * i + c0_ + nn], True, False,
               ['qzt%d' % qb, 'kT'], [sbk], inc=False)
            last = not (i < 4 and c0_ == 0)
            mm(o_, identb[:], Thi[:, hd, c0_:c0_ + nn], False, last, ['identb', 'Thi%d' % hd], [sbk], inc=(c0_ == 512))
            if not last:
                w_ = 512 - 128 * i
                mm(bank(bk)[:, 0:w_], identb[:], hb8[:, 128 * i:512], False, True, ['identb', 'hb8'], [sbk], inc=False)
        sv = AP(ps, ps[:, 2 * a_, 0:1].offset, [[psrow, 128], [1, 640]])
        R.op('dve', lambda e, sv=sv, hd=hd: e.reduce_max(out=stat[:, hd, 1:2], in_=sv, axis=AX.X, negate=True), [sbk], ['st%d' % hd])

    def S2a(n):
        i, hd = n // 8, n % 8
        a_ = n % 2
        sv = AP(ps, ps[:, 2 * a_, 0:1].offset, [[psrow, 128], [1, 640]])
        act(Pf[a_][:], sv, AF.Exp, ['pss%d' % a_, 'st%d' % hd], ['Pf%d' % a_, 'sr%d' % hd], scale=1.0, bias=stat[:, hd, 1:2],
            accum=stat[:, hd, 2:3])
        R.op('dve', lambda e, hd=hd: e.reciprocal(out=stat[:, hd, 3:4], in_=stat[:, hd, 2:3]), ['sr%d' % hd], ['sr%d' % hd])

    def S2b(n):
        a_ = n % 2
        for kt in range(5):
            R.op('pe', lambda e, a_=a_, kt=kt: e.transpose(ptb[a_][:, kt * 128:(kt + 1) * 128], Pf[a_][:, kt * 128:(kt + 1) * 128], identb[:]),
                 ['Pf%d' % a_, 'identb'], ['pst%d' % a_], inc=(kt == 4))
        R.op('act', lambda e, a_=a_: e.copy(out=PT[a_][:].rearrange("p a b -> p (a b)"), in_=ptb[a_][:, 0:640]),
             ['pst%d' % a_], ['PT%d' % a_])

    def S3(n):
        i, hd = n // 8, n % 8
        a_ = n % 2
        ob = i % 2
        for kt in range(5):
            mm(bank(6 + a_)[:, 0:64], PT[a_][:, kt, :], Vt[:, i + kt, hd * 64:(hd + 1) * 64], kt == 0, kt == 4,
               ['PT%d' % a_, 'Vt'], ['pso%d' % a_])
        ts('dve', Oat[ob][:, hd * 64:(hd + 1) * 64], bank(6 + a_)[:, 0:64], stat[:, hd, 3:4], None, ALU.mult, None,
           ['pso%d' % a_, 'sr%d' % hd], ['Oat%d' % ob])

    def S4(i):
        ob = i % 2
        ok = 'Oat%d' % ob; nk = 'On%d' % ob
        act(On[ob][:], Oat[ob][:], AF.Square, [ok], [nk, 'ast%d' % ob], accum=ast[:, ob:ob + 1])
        act(ast[:, ob:ob + 1], ast[:, ob:ob + 1], AF.Ln, ['ast%d' % ob, 'epsc'], ['ast%d' % ob], scale=1.0 / 512, bias=epsc[:])
        act(ast[:, ob:ob + 1], ast[:, ob:ob + 1], AF.Exp, ['ast%d' % ob], ['ast%d' % ob], scale=-0.5)
        ts('dve', On[ob][:], Oat[ob][:], ast[:, ob:ob + 1], None, ALU.mult, None, [ok, 'ast%d' % ob], [nk])
        for ct in range(4):
            tr(bank(7)[:, 128 + ct * 96:128 + ct * 96 + 96] if False else bank(7)[:, ct * 128:(ct + 1) * 128],
               On[ob][:, ct * 128:(ct + 1) * 128], ident[:], [nk, 'ident'], ['pso1'])
        for ct in range(4):
            ts('dve', mixT[:, 4 + ct, 128 * i:128 * i + 128], bank(7)[:, ct * 128:(ct + 1) * 128], ga[:, ct:ct + 1], None,
               ALU.mult, None, ['pso1', 'ga'], ['mixT'])

    NU = 128
    for n in range(NU + 8):
        if n < NU:
            S1(n)
        if 0 <= n - 1 < NU:
            S2a(n - 1)
        if 0 <= n - 2 < NU:
            S2b(n - 2)
        if 0 <= n - 3 < NU:
            S3(n - 3)
        m_ = n - 3 - 4
        if m_ >= 7 and m_ % 8 == 7 and m_ < NU:
            S4(m_ // 8)
    R.emit("p3")
    p3.close()
    p2.close()

    wy = sb("wy", [128, 2, 16, 256], BF16, w4)
    dma('sp', wy[:].rearrange("p a b c -> p (a b c)"), wsp.ap()[:, 8192:16384], ['wsp'], ['wy'], 'wyL')

    p4 = Scope()
    yT = sb("yT", [128, 4, TOK], F32, p4)
    p4a = Scope()
    Yb = sb("Yb", [128, 16, 8, 16], F32, p4a)
    urow = uT[:].ap[0][0]
    for chk in range(4):
        for ct in range(4):
            bk = (chk * 4 + ct) % 4
            for tau in range(16):
                if tau == 0:
                    o = bank(bk); rhs = uT[:, ct, chk * 512:(chk + 1) * 512]
                else:
                    o = AP(ps, ps[:, bk, tau:tau + 1].offset, [[ps[:].ap[0][0], 128], [16, 32], [1, 16 - tau]])
                    rhs = AP(uT, uT[:, ct, chk * 512:chk * 512 + 1].offset, [[urow, 128], [16, 32], [1, 16 - tau]])
                mm(o, wfir[:, tau, ct, :], rhs, tau == 0, tau == 15, ['wfir', 'uT'], ['psb%d' % bk])
            R.op('act', lambda e, ct=ct, chk=chk, bk=bk: e.copy(out=yT[:, ct, chk * 512:(chk + 1) * 512], in_=bank(bk)),
                 ['psb%d' % bk], ['yT%d' % q_ for q_ in range(8)])
    yrow = yT[:].ap[0][0]
    for ct in range(4):
        for gl in range(8):
            g = 8 * ct + gl
            pair, g2 = g // 2, g % 2
            bk = 4 + g % 4
            hs = slice(64 * g2, 64 * g2 + 64)
            for c in range(2):
                mm(bank(bk)[:, 0:256], sprev[hs, c, pair, :], wy[hs, c, pair, :], c == 0, c == 1, ['sprev', 'wy'], ['psb%d' % bk],
                   tp=(64 * g2, 0))
            R.op('act' if g % 2 else 'dve', (lambda e, gl=gl, bk=bk: (e.copy if hasattr(e, 'copy') else e.tensor_copy)(
                out=Yb[:, :, gl, :], in_=bank(bk)[:, 0:256].rearrange("p (a b) -> p a b", b=16))), ['psb%d' % bk], ['Yb'])
        for ib in range(4):
            bk = ib
            for q4 in range(4):
                i = 4 * ib + q4
                R.op('pe', lambda e, bk=bk, q4=q4, i=i: e.transpose(bank(bk)[:, q4 * 128:(q4 + 1) * 128],
                                                                   Yb[:, i, :, :].rearrange("p a b -> p (a b)"), ident[:]),
                     ['Yb', 'ident'], ['psb%d' % bk], inc=(q4 == 3))
            yv = AP(yT, yT[:, ct, 4 * ib:4 * ib + 1].offset, [[yrow, 128], [1, 4], [16, 128]])
            tt('dve', yv, yv, bank(bk).rearrange("p (a b) -> p a b", a=4), ALU.add,
               ['psb%d' % bk] + ['yT%d' % q_ for q_ in range(8)], ['yT%d' % q_ for q_ in range(8)])
    R.emit("p4a")
    p4a.close()
    w4.close()
    wob = sb("wob", [128, 8, D], BF16)
    wdb = sb("wdb", [128, NF, D], BF16)
    for k in range(8):
        dma('pool', wob[:, k, :], w_out.ap()[k * 128:(k + 1) * 128, :], [], ['wob'], '@wob')
    R.seal('wob')
    for h in range(2):
        dma('sp', wdb[:, 11 * h:11 * h + 11, :], AP(wd_s, h * 11 * 128 * D, [[D, 128], [128 * D, 11], [1, D]]),
            ['wd_s0', 'wd_s1'], ['wdb'], 'wdb%d' % h)
    ygb = sb("ygb", [128, 4, TOK], BF16, p4)
    tg = sb("tg", [128, 4, 256], F32, p4)
    sqb = [sb("sqb%d" % i, [128, 4, 256], BF16, p4) for i in range(2)]
    ssb = sb("ssb", [128, TOK], F32, p4)
    onesb = sb("onesb", [128, 128], BF16, p4)
    hbg = sb("hbg", [128, 4], F32, p4)
    lnh = sb("lnh", [128, 1], F32, p4)
    wgl = sb("wgl", [128, 4, 512], BF16, p4)
    for k in range(4):
        dma('pool', wgl[:, k, :], w_glu.ap()[k * 128:(k + 1) * 128, :], [], ['wgl'], '@wgl')
    R.seal('wgl')
    R.op('dve', lambda e: e.tensor_copy(out=onesb[:], in_=ones[:]), ['ones'], ['onesb'])
    R.op('dve', lambda e: e.memset(lnh[:], math.log(0.5)), [], ['lnh'])
    ts('dve', hbg[:], bg[:], 0.5, None, ALU.mult, None, ['bg'], ['hbg'])
    for chk in range(8):
        cs_ = slice(chk * 256, (chk + 1) * 256)
        for ct in range(4):
            act(yT[:, ct, cs_], yT[:, ct, cs_], AF.Gelu, ['yT%d' % chk], ['yT%d' % chk])
            R.op('dve', lambda e, ct=ct, cs_=cs_: e.tensor_copy(out=ygb[:, ct, cs_], in_=yT[:, ct, cs_]), ['yT%d' % chk], ['ygb%d' % chk])

    def Ymm(chk):
        cs_ = slice(chk * 256, (chk + 1) * 256)
        p_ = chk % 2
        for co in range(4):
            bk = 3 * p_ + co // 2
            for k in range(4):
                mm(bank(bk)[:, (co % 2) * 256:(co % 2) * 256 + 256], wgl[:, k, co * 128:(co + 1) * 128], ygb[:, k, cs_], k == 0, k == 3,
                   ['wgl', 'ygb%d' % chk], ['psb%d' % bk])

    def Yrest(chk):
        cs_ = slice(chk * 256, (chk + 1) * 256)
        b_ = chk % 2
        for co in range(4):
            bk = 3 * b_ + co // 2
            act(tg[:, co, :], bank(bk)[:, (co % 2) * 256:(co % 2) * 256 + 256], AF.Tanh, ['psb%d' % bk, 'hbg'], ['tg%d' % co], scale=0.5,
                bias=hbg[:, co:co + 1])
            R.op('dve', lambda e, co=co, cs_=cs_: e.scalar_tensor_tensor(out=yT[:, co, cs_], in0=tg[:, co, :], scalar=1.0, in1=yT[:, co, cs_],
                                                                       op0=ALU.add, op1=ALU.mult), ['tg%d' % co, 'yT%d' % chk], ['yT%d' % chk])
            tt('dve', sqb[b_][:, co, :], yT[:, co, cs_], yT[:, co, cs_], ALU.mult, ['yT%d' % chk], ['sq%d_%d' % (b_, co)])
        for co in range(4):
            mm(bank(3 * b_ + 2)[:, 0:256], onesb[:], sqb[b_][:, co, :], co == 0, co == 3, ['onesb', 'sq%d_%d' % (b_, co)], ['psb%d' % (3 * b_ + 2)])
        R.op('act', lambda e, b_=b_, cs_=cs_: e.copy(out=ssb[:, cs_], in_=bank(3 * b_ + 2)[:, 0:256]), ['psb%d' % (3 * b_ + 2)], ['ssb'])

    Ymm(0)
    for chk in range(8):
        if chk + 1 < 8:
            Ymm(chk + 1)
        Yrest(chk)
    act(ssb[:], ssb[:], AF.Ln, ['ssb', 'epsc'], ['ssb'], scale=1.0 / 2048, bias=epsc[:])
    act(ssb[:], ssb[:], AF.Exp, ['ssb', 'lnh'], ['ssb'], scale=-0.5, bias=lnh[:])
    for chk in range(4):
        cs_ = slice(chk * 512, (chk + 1) * 512)
        for co in range(4):
            R.op('dve', lambda e, co=co, cs_=cs_: e.scalar_tensor_tensor(out=mixT[:, co, cs_], in0=yT[:, co, cs_], scalar=gs[:, co:co + 1],
                                                                       in1=ssb[:, cs_], op0=ALU.mult, op1=ALU.mult),
                 ['yT%d' % (2 * chk), 'yT%d' % (2 * chk + 1), 'ssb', 'gs'], ['mixT'])
    R.emit("p4b")
    p4.close()
    ust.close()

    p5 = Scope()
    wgr = [sb("wgr%d" % i, [128, 8, 256], BF16, p5) for i in range(3)]
    wur = [sb("wur%d" % i, [128, 8, 256], BF16, p5) for i in range(3)]
    x1 = sb("x1", [128, 4, D], F32, p5)
    xr = [sb("xr%d" % i, [128, D], F32, p5) for i in range(2)]
    gffb = sb("gffb", [128, D], F32, p5)
    h2T = sb("h2T", [128, 8, 512], BF16, p5)
    actT = sb("actT", [128, NF, 512], BF16, p5)
    sl = [sb("sl%d" % i, [128, 512], F32, p5) for i in range(2)]
    gfb = sb("gfb", [128, D], F32, p5)
    fst = sb("fst", [128, 4], F32, p5)
    xn5 = sb("xn5", [128, D], F32, p5)
    junk5 = xn5
    ot = [sb("ot%d" % i, [128, D], F32, p5) for i in range(2)]
    xissued = set()

    def P1a(ti):
        if ti in xissued or ti >= 16:
            return
        xissued.add(ti)
        xi = ti % 2
        dma('sp', xr[xi][:], xw.ap()[WIN - TOK + ti * 128:WIN - TOK + (ti + 1) * 128, :], [], ['xr%d' % xi], 'xr%d' % xi)

    P1a(0)
    P1a(1)
    dma('sp', gfb[:], AP(g_fin, 0, [[0, 128], [1, D]]), [], ['gfb'], 'gfb')
    dma('sp', gffb[:], AP(g_ffn, 0, [[0, 128], [1, D]]), [], ['gffb'], 'gffb')
    wslot = 0
    xcount = 0
    xb5 = sb("xb5", [128, D], BF16, p5)
    identb5 = sb("identb5", [128, 128], BF16, p5)
    R.op('dve', lambda e: e.tensor_copy(out=identb5[:], in_=ident[:]), ['ident'], ['identb5'])
    pt5 = [ps[:, 2 + a_, :].bitcast(BF16) for a_ in range(2)]

    def P1(ti, tl):
        P1a(ti)
        xi = ti % 2
        for half in range(2):
            bk = 2 * tl + half
            for k in range(8):
                mm(bank(bk), mixT[:, k, ti * 128:(ti + 1) * 128], wob[:, k, half * 512:(half + 1) * 512], k == 0, k == 7,
                   ['mixT', 'wob'], ['psb%d' % bk])
        for half in range(2):
            bk = 2 * tl + half
            tt('dve', x1[:, tl, half * 512:(half + 1) * 512], bank(bk), xr[xi][:, half * 512:(half + 1) * 512], ALU.add,
               ['psb%d' % bk, 'xr%d' % xi], ['x1_%d' % tl])
        P1a(ti + 2)

    def P2(ti, tl):
        R.op('dve', lambda e, tl=tl: e.scalar_tensor_tensor(out=xn5[:], in0=x1[:, tl, :], scalar=1.0, in1=x1[:, tl, :],
                                                          op0=ALU.mult, op1=ALU.mult, accum_out=fst[:, 0:1]), ['x1_%d' % tl], ['xn5', 'fst0'])
        act(fst[:, 0:1], fst[:, 0:1], AF.Sqrt, ['fst0', 'epsc'], ['fst0'], scale=1.0 / D, bias=epsc[:])
        R.op('dve', lambda e, o_=fst[:, 0:1]: e.reciprocal(out=o_, in_=o_), ['fst0'], ['fst0'])
        R.op('dve', lambda e, tl=tl: e.scalar_tensor_tensor(out=xb5[:], in0=x1[:, tl, :], scalar=fst[:, 0:1], in1=gffb[:],
                                                          op0=ALU.mult, op1=ALU.mult), ['x1_%d' % tl, 'fst0', 'gffb'], ['xb5'])
        bk = 2 * tl
        ptv = ps[:, bk, :].bitcast(BF16)
        for k in range(8):
            R.op('pe', lambda e, k=k, ptv=ptv: e.transpose(ptv[:, k * 128:(k + 1) * 128], xb5[:, k * 128:(k + 1) * 128], identb5[:]),
                 ['xb5', 'identb5'], ['psb%d' % bk], inc=(k == 7))
        src = ptv.rearrange("p (a b) -> p a b", a=8)
        dst = h2T[:, :, tl * 128:(tl + 1) * 128]
        R.op('act', lambda e, src=src, dst=dst: e.copy(out=dst, in_=src), ['psb%d' % bk], ['h2T'])

    wl_next = [0]

    def wload(upto):
        while wl_next[0] <= min(upto, 43):
            g = wl_next[0]
            fp_ = g % 11
            s_ = g % 3
            dma('sp', wgr[s_][:], AP(wg_s, fp_ * 256, [[DFF, 128], [128 * DFF, 8], [1, 256]]), ['wg_s0', 'wg_s1'], ['wgr%d' % s_], 'wgr%d' % s_)
            dma('sp', wur[s_][:], AP(wu_s, fp_ * 256, [[DFF, 128], [128 * DFF, 8], [1, 256]]), ['wu_s0', 'wu_s1'], ['wur%d' % s_], 'wur%d' % s_)
            wl_next[0] += 1

    wload(1)
    for blk in range(4):
        for tl in range(4):
            P1(blk * 4 + tl, tl)
        for tl in range(4):
            P2(blk * 4 + tl, tl)
        for fp in range(11):
            s = (blk * 11 + fp) % 3
            wload(blk * 11 + fp + 2)
            for fi in range(2):
                f = fp * 2 + fi
                pb_ = 4 + 2 * (f % 2)
                for k in range(8):
                    mm(bank(pb_), wgr[s][:, k, fi * 128:(fi + 1) * 128], h2T[:, k, :], k == 0, k == 7, ['wgr%d' % s, 'h2T'], ['psb%d' % pb_])
                for k in range(8):
                    mm(bank(pb_ + 1), wur[s][:, k, fi * 128:(fi + 1) * 128], h2T[:, k, :], k == 0, k == 7, ['wur%d' % s, 'h2T'],
                       ['psb%d' % (pb_ + 1)])
                act(sl[f % 2][:], bank(pb_), AF.Silu, ['psb%d' % pb_], ['sl%d' % (f % 2)])
                tt('dve', actT[:, f, :], sl[f % 2][:], bank(pb_ + 1), ALU.mult, ['sl%d' % (f % 2), 'psb%d' % (pb_ + 1)], ['actT'])
        P1a((blk + 1) * 4)
        P1a((blk + 1) * 4 + 1)
        for tl in range(4):
            ti = blk * 4 + tl
            oi = ti % 2
            for half in range(2):
                for f in range(NF):
                    mm(bank(half), actT[:, f, tl * 128:(tl + 1) * 128], wdb[:, f, half * 512:(half + 1) * 512], f == 0, f == NF - 1,
                       ['actT', 'wdb'], ['psb%d' % half])
                tt('dve', x1[:, tl, half * 512:(half + 1) * 512], bank(half), x1[:, tl, half * 512:(half + 1) * 512], ALU.add,
                   ['psb%d' % half, 'x1_%d' % tl], ['x1_%d' % tl])
            act(junk5[:], x1[:, tl, :], AF.Square, ['x1_%d' % tl], ['xn5', 'fst1'], accum=fst[:, 1:2])
            ts('dve', fst[:, 1:2], fst[:, 1:2], 1.0 / D, EPS, ALU.mult, ALU.add, ['fst1'], ['fst1'])
            act(fst[:, 1:2], fst[:, 1:2], AF.Sqrt, ['fst1'], ['fst1']); R.op('dve', lambda e, o_=fst[:, 1:2]: e.reciprocal(out=o_, in_=o_), ['fst1'], ['fst1'])
            R.op('dve', lambda e, tl=tl, oi=oi: e.scalar_tensor_tensor(out=ot[oi][:], in0=x1[:, tl, :], scalar=fst[:, 1:2], in1=gfb[:],
                                                                      op0=ALU.mult, op1=ALU.mult), ['x1_%d' % tl, 'fst1', 'gfb'], ['ot%d' % oi])
            dma('pool', out.ap()[ti * 128:(ti + 1) * 128, :], ot[oi][:], ['ot%d' % oi], ['out%d' % oi], 'ot%d' % oi)
    R.op('sp', None, ['out0', 'out1'], [])
    R.emit("p5")
    p5.close()
    st.close()
    return nc


_NC = None


def kernel(**inputs):
    global _NC
    x = np.ascontiguousarray(np.asarray(inputs["x"], dtype=np.float32))
    B, S, _ = x.shape
    shared = {}
    for k, v in inputs.items():
        if k == "x":
            continue
        a = np.ascontiguousarray(np.asarray(v, dtype=np.float32))
        if a.ndim >= 2 and a.shape[0] == 1:
            a = a.reshape(a.shape[1:])
        if k == "ssm_d":
            a = a.reshape(512)
        shared[k] = np.ascontiguousarray(a)
    ident = np.eye(128, dtype=np.float32)
    bdm = np.kron(np.eye(8, dtype=np.float32), np.ones((16, 16), np.float32))
    ql = np.arange(128)[:, None]; kc = np.arange(640)[None, :]
    band_idx = kc // 64 - ql // 64
    band = np.where((band_idx >= 0) & (band_idx <= 8), 0.0, -1e30).astype(np.float32)
    shared["c_ident"] = ident; shared["c_bdm"] = bdm; shared["c_band"] = band
    in_maps = []
    for c in range(8):
        b, q = c // 4, c % 4
        end = (q + 1) * TOK
        xwin = np.zeros((WIN, D), np.float32)
        xwin[WIN - end:] = x[b, :end]
        m = dict(shared)
        m["xw"] = xwin
        m["c_hb"] = np.full((128, 512), -1e30 if q == 0 else 0.0, np.float32)
        in_maps.append(m)
    if _NC is None:
        _NC = build_program()
    res = run_bass_kernel_spmd(_NC, in_maps, core_ids=list(range(8)))
    outp = np.zeros((B, S, D), np.float32)
    for c in range(8):
        b, q = c // 4, c % 4
        outp[b, q * TOK:(q + 1) * TOK] = res.results[c]["out"]
    return outp
```

```python
import math
from contextlib import ExitStack
import numpy as np
import concourse.bass as bass
import concourse.mybir as mybir
from concourse.bass_utils import run_bass_kernel_spmd

F32 = mybir.dt.float32
BF16 = mybir.dt.bfloat16
I32 = mybir.dt.int32
AF = mybir.ActivationFunctionType
ALU = mybir.AluOpType
AX = mybir.AxisListType

D = 1024
TOK = 2048
WIN = 8192
DFF = 2816
NF = 22
EPS = 1e-6
PI = math.pi


class Rec:
    def __init__(self, nc, stack):
        self.nc = nc
        self.stack = stack
        self.engs = ['pe', 'act', 'dve', 'pool', 'sp']
        self.sem = {}
        self.cnt = {}
        self.lastw = {}
        self.readers = {}
        self.waited = {e: {} for e in self.engs}
        self.ops = {e: [] for e in self.engs}
        self.pending_pe = []
        self.groups = {}

    def seal(self, grp):
        sname = 'd_' + grp
        for tok in self.groups.pop(grp, []):
            tok[1] = self.cnt[sname]

    def get_sem(self, name):
        if name not in self.sem:
            self.sem[name] = self.stack.enter_context(self.nc.semaphore(name))
            self.cnt[name] = 0
        return self.sem[name]

    def op(self, eng, fn, reads=(), writes=(), inc=True, dma=None):
        deps = []
        ps_reads = [k for k in reads if k.startswith('ps')]
        writes = list(writes) + ps_reads
        for k in reads:
            if k in self.lastw:
                deps.append(self.lastw[k])
        for k in writes:
            if k in self.lastw:
                deps.append(self.lastw[k])
            deps.extend(self.readers.get(k, {}).values())
        waits = {}
        for sname, val in deps:
            if eng == 'pe' and sname == 'pe':
                continue
            if val is None and dma is not None and dma.startswith('@') and sname == 'd_' + dma[1:]:
                continue
            assert val is not None, "dependency on unresolved PE op"
            if self.waited[eng].get(sname, 0) >= val:
                continue
            waits[sname] = max(waits.get(sname, 0), val)
        for s, v in waits.items():
            self.waited[eng][s] = v
        if dma is not None and dma.startswith('@'):
            sname = 'd_' + dma[1:]
            self.get_sem(sname)
            self.cnt[sname] += 16
            tok = [sname, None]
            self.groups.setdefault(dma[1:], []).append(tok)
            incspec = (sname, 16)
        elif dma is not None:
            sname = 'd_' + dma
            self.get_sem(sname)
            self.cnt[sname] += 16
            tok = [sname, self.cnt[sname]]
            incspec = (sname, 16)
        elif inc:
            self.get_sem(eng)
            self.cnt[eng] += 1
            tok = [eng, self.cnt[eng]]
            incspec = (eng, 1)
            if eng == 'pe':
                for p in self.pending_pe:
                    p[1] = tok[1]
                self.pending_pe = []
        else:
            assert eng == 'pe'
            tok = ['pe', None]
            self.pending_pe.append(tok)
            incspec = None
        for k in reads:
            self.readers.setdefault(k, {})[(eng, tok[0])] = tok
        for k in writes:
            self.lastw[k] = tok
            self.readers[k] = {}
        self.ops[eng].append((sorted(waits.items()), fn, incspec))

    def emit(self, name):
        assert not self.pending_pe
        with self.nc.Block(name) as blk:
            for e, deco in (('pe', blk.tensor), ('act', blk.scalar), ('dve', blk.vector),
                            ('pool', blk.gpsimd), ('sp', blk.sync)):
                ops = self.ops[e]

                def body(engh, ops=ops):
                    for waits, fn, incspec in ops:
                        for s, v in waits:
                            engh.wait_ge(self.sem[s], v)
                        if fn is None:
                            continue
                        ins = fn(engh)
                        if incspec:
                            ins.then_inc(self.sem[incspec[0]], incspec[1])
                deco(body)
                self.ops[e] = []


def AP(t, off, dims):
    return bass.AP(t.tensor if hasattr(t, 'tensor') else t, off, [list(d) for d in dims])


def build_program():
    nc = bass.Bass("TRN2", target_bir_lowering=False)
    din = {}

    def inp(name, shape):
        din[name] = nc.dram_tensor(name, list(shape), F32, kind="ExternalInput")
        return din[name]

    xw = inp("xw", [WIN, D])
    g_mix = inp("norm_mix_g", [D]); w_in = inp("w_in", [D, 2048])
    lam_re = inp("ssm_lam_re", [32, 64]); lam_im = inp("ssm_lam_im", [32, 64]); log_dt = inp("ssm_log_dt", [32])
    b_re = inp("ssm_b_re", [32, 64, 16]); b_im = inp("ssm_b_im", [32, 64, 16])
    c_re = inp("ssm_c_re", [32, 16, 64]); c_im = inp("ssm_c_im", [32, 16, 64])
    d_sk = inp("ssm_d", [512]); w_glu = inp("ssm_w_glu", [512, 512]); b_glu = inp("ssm_b_glu", [512])
    relb = inp("attn_rel_bias", [8, 257])
    g_ssm = inp("norm_ssm_out_g", [512]); g_att = inp("norm_att_out_g", [512])
    w_out = inp("w_out", [D, D]); g_ffn = inp("norm_ffn_g", [D])
    w_gate = inp("w_gate", [D, DFF]); w_up = inp("w_up", [D, DFF]); w_down = inp("w_down", [DFF, D])
    g_fin = inp("norm_final_g", [D])
    c_ident = inp("c_ident", [128, 128]); c_bdm = inp("c_bdm", [128, 128])
    c_band = inp("c_band", [128, 640]); c_hb = inp("c_hb", [128, 512])
    out = nc.dram_tensor("out", [TOK, D], F32, kind="ExternalOutput")
    wg_s = nc.dram_tensor("wg_s", [D, DFF], BF16, kind="Internal")
    wu_s = nc.dram_tensor("wu_s", [D, DFF], BF16, kind="Internal")
    wd_s = nc.dram_tensor("wd_s", [DFF, D], BF16, kind="Internal")
    relE = nc.dram_tensor("relE", [8, 768], F32, kind="Internal")
    relE2 = nc.dram_tensor("relE2", [8, 128 * 768], F32, kind="Internal")

    st = ExitStack()
    R = Rec(nc, st)

    AF_EL = 12880
    AB_EL = 80640
    arena = {F32: st.enter_context(nc.sbuf_tensor("arena_f", [128, AF_EL], F32)),
             BF16: st.enter_context(nc.sbuf_tensor("arena_b", [128, AB_EL], BF16))}
    free = {F32: [[0, AF_EL]], BF16: [[0, AB_EL]]}

    class Scope:
        def __init__(self):
            self.items = []

        def close(self):
            for dt, off, n in self.items:
                fl = free[dt]
                fl.append([off, off + n])
                fl.sort()
                m = [fl[0]]
                for a, b in fl[1:]:
                    if a <= m[-1][1]:
                        m[-1][1] = max(m[-1][1], b)
                    else:
                        m.append([a, b])
                free[dt] = m
            self.items = []

    pst = Scope()

    def sb(name, shape, dt=F32, stack=None):
        stack = pst if stack is None else stack
        n = int(np.prod(shape[1:]))
        n = (n + 15) // 16 * 16
        for seg in free[dt]:
            if seg[1] - seg[0] >= n:
                off = seg[0]
                seg[0] += n
                break
        else:
            raise RuntimeError("arena full for %s %s %s" % (name, shape, free[dt]))
        stack.items.append((dt, off, n))
        v = arena[dt][:, off:off + int(np.prod(shape[1:]))]
        if len(shape) == 2:
            return v
        names = "abcdefg"[:len(shape) - 1]
        pat = "p (%s) -> p %s" % (" ".join(names), " ".join(names))
        return v.rearrange(pat, **{names[i]: shape[1 + i] for i in range(len(shape) - 2)})

    def dma(eng, o, i, reads, writes, slot):
        R.op(eng, lambda e: e.dma_start(out=o, in_=i, allow_slow_non_contiguous=True), reads, writes, dma=slot)

    def tt(eng, o, a, b, op, r, w):
        R.op(eng, lambda e: e.tensor_tensor(out=o, in0=a, in1=b, op=op), r, w)

    def ts(eng, o, a, s1, s2, op0, op1, r, w):
        if s2 is None:
            R.op(eng, lambda e: e.tensor_scalar(out=o, in0=a, scalar1=s1, scalar2=None, op0=op0), r, w)
        else:
            R.op(eng, lambda e: e.tensor_scalar(out=o, in0=a, scalar1=s1, scalar2=s2, op0=op0, op1=op1), r, w)

    def act(o, i, func, r, w, scale=1.0, bias=None, accum=None):
        kw = {}
        if bias is not None:
            kw['bias'] = bias
        if accum is not None:
            kw['accum_out'] = accum
        R.op('act', lambda e: e.activation(out=o, in_=i, func=func, scale=scale, **kw), r, w)

    def mm(o, lhsT, rhs, start, stop, r, w, tp=None, inc=None):
        kw = {}
        if tp is not None:
            kw['tile_position'] = tp
        R.op('pe', lambda e: e.matmul(o, lhsT, rhs, start=start, stop=stop, **kw), r, w, inc=stop if inc is None else inc)

    def tr(o, i, ident, r, w):
        R.op('pe', lambda e: e.transpose(o, i, ident), r, w)

    def cmul(eng, ore, oim, are, aim, bre, bim, t1, t2, r, w, tk, conj=False):
        tt(eng, t1, are, bre, ALU.mult, r, [tk + '1'])
        tt(eng, t2, aim, bim, ALU.mult, r, [tk + '2'])
        tt(eng, ore, t1, t2, ALU.add if conj else ALU.subtract, [tk + '1', tk + '2'], w)
        tt(eng, t1, aim, bre, ALU.mult, r, [tk + '1'])
        tt(eng, t2, are, bim, ALU.mult, r, [tk + '2'])
        tt(eng, oim, t1, t2, ALU.subtract if conj else ALU.add, [tk + '1', tk + '2'], w)

    ps = st.enter_context(nc.psum_tensor("ps", [128, 8, 512], F32))

    def bank(b):
        return ps[:, b, :]

    ident = sb("ident", [128, 128]); bdm = sb("bdm", [128, 128])
    vecs = sb("vecs", [128, 32])
    gm = vecs[:, 0:8]; gf = vecs[:, 8:16]; gs = vecs[:, 16:20]; ga = vecs[:, 20:24]
    bg = vecs[:, 24:28]; dk = vecs[:, 28:32]
    carry = sb("carry", [128, 2, 16]); send = sb("send", [128, 2, 16])
    r16 = sb("r16", [128, 16])
    ones = sb("ones", [128, 128])
    epsc = sb("epsc", [128, 1])

    dma('sp', ident[:], c_ident.ap(), [], ['ident'], '@cst')
    dma('sp', bdm[:], c_bdm.ap(), [], ['bdm'], '@cst')
    R.seal('cst')
    R.op('pool', lambda e: e.memset(ones[:], 1.0), [], ['ones'])
    R.op('pool', lambda e: e.memset(carry[:], 0.0), [], ['carry'])
    R.op('pool', lambda e: e.memset(epsc[:], EPS), [], ['epsc'])

    wst = Scope()
    winb = sb("winb", [128, 8, 2048], BF16, wst)
    wz = sb("wz", [128, 4, 16, 2, 128], BF16, wst)
    u16c = sb("u16c", [128, 16]); u16s = sb("u16s", [128, 16])
    wsp = nc.dram_tensor("wsp", [128, 16384], BF16, kind="Internal")
    p0 = Scope()
    p0t = Scope()
    tabS = nc.dram_tensor("tabS", [128, 1824], F32, kind="Internal")
    TABS = (("Bb", [128, 2, 16, 16], 0, 512), ("CT", [128, 2, 16, 16], 512, 512), ("CTim", [128, 16, 16], 1024, 256),
            ("Pre", [128, 17, 16], 1280, 272), ("Pim", [128, 17, 16], 1552, 272))
    Bb = sb("Bb", [128, 2, 16, 16], F32, p0)
    CT = sb("CT", [128, 2, 16, 16], F32, p0)
    CTim = sb("CTim", [128, 16, 16], F32, p0)
    Pre = sb("Pre", [128, 17, 16], F32, p0); Pim = sb("Pim", [128, 17, 16], F32, p0)
    lre = sb("lre", [128, 16], F32, p0); lim = sb("lim", [128, 16], F32, p0); ldt = sb("ldt", [128, 16], F32, p0)
    Bsp = sb("Bsp", [128, 2, 16, 16], F32, p0t)
    Cn = sb("Cn", [128, 2, 4, 2, 64], F32, p0t)
    lrow = sb("lrow", [128, 2, 2, 64], F32, p0t)[0:32]
    ldc = sb("ldc", [128, 1], F32, p0t)[0:32]
    dgl = sb("dgl", [128, 32], F32, p0t)[0:32]
    vrow = sb("vrow", [128, 128], F32, p0t)[0:32]
    dma('sp', lrow[:, 0, :, :], AP(lam_re, 0, [[64, 32], [0, 2], [1, 64]]), [], ['lrow'], '@ssmin')
    dma('sp', lrow[:, 1, :, :], AP(lam_im, 0, [[64, 32], [0, 2], [1, 64]]), [], ['lrow'], '@ssmin')
    dma('sp', ldc[:], AP(log_dt, 0, [[1, 32], [1, 1]]), [], ['ldc'], '@ssmin')
    for g2 in range(2):
        hs = slice(64 * g2, 64 * g2 + 64)
        dma('sp', Bsp[hs, 0, :, :], AP(b_re, g2 * 1024, [[16, 64], [2048, 16], [1, 16]]), [], ['Bsp'], '@ssmin')
        dma('sp', Bsp[hs, 1, :, :], AP(b_im, g2 * 1024, [[16, 64], [2048, 16], [1, 16]]), [], ['Bsp'], '@ssmin')
    for ci, csrc in enumerate((c_re, c_im)):
        for ct in range(4):
            dma('sp', Cn[:, ci, ct, :, :], AP(csrc, ct * 8192, [[64, 128], [0, 2], [1, 64]]), [], ['Cn'],
                '@ssmin')
    R.seal('ssmin')
    psrow0 = ps[:].ap[0][0]
    for c_, dst_, key_ in ((0, lre, 'lre'), (1, lim, 'lim')):
        tr(bank(2 + c_)[:, 0:32], lrow[:, c_, :, :].rearrange("p a b -> p (a b)"), ident[0:32, 0:32], ['lrow', 'ident'], ['psb%d' % (2 + c_)])
        for g2 in range(2):
            hs = slice(64 * g2, 64 * g2 + 64)
            src_ = AP(ps, ps[hs, 2 + c_, g2:g2 + 1].offset, [[psrow0, 64], [2, 16]])
            R.op('dve', lambda e, src_=src_, hs=hs, dst_=dst_: e.tensor_copy(out=dst_[hs, :], in_=src_), ['psb%d' % (2 + c_)], [key_])
    ts('dve', dgl[:], ident[0:32, 0:32], ldc[:, 0:1], None, ALU.mult, None, ['ident', 'ldc'], ['dgl'])
    mm(bank(4)[:, 0:32], ones[0:32, :], dgl[:], True, True, ['ones', 'dgl'], ['psb4'])
    for g2 in range(2):
        hs = slice(64 * g2, 64 * g2 + 64)
        src_ = AP(ps, ps[hs, 4, g2:g2 + 1].offset, [[psrow0, 64], [2, 16]])
        R.op('dve', lambda e, src_=src_, hs=hs: e.tensor_copy(out=ldt[hs, :], in_=src_), ['psb4'], ['ldt'])
    r0 = 0
    for (src, n) in ((g_mix, 8), (g_ffn, 8), (g_ssm, 4), (g_att, 4), (b_glu, 4), (d_sk, 4)):
        dma('sp', vrow[r0:r0 + n, :], AP(src, 0, [[128, n], [1, 128]]), [], ['vrow'], '@cst2')
        r0 += n
    R.seal('cst2')
    tr(bank(5)[:, 0:32], vrow[:], ident[0:32, 0:32], ['vrow', 'ident'], ['psb5'])
    R.op('dve', lambda e: e.tensor_copy(out=vecs[:], in_=bank(5)[:, 0:32]), ['psb5'], ['gm', 'gf', 'gs', 'ga', 'bg', 'dk'])
    wstg = Scope()
    stg = [sb("stg%d" % i, [128, 4096], BF16, wstg) for i in range(2)]
    w_in_b = w_in.ap().bitcast(BF16)
    for k in range(8):
        sl_ = k % 2
        dma('sp', stg[sl_][:], w_in_b[k * 128:(k + 1) * 128, :], [], ['stg%d' % sl_], 'stg%d' % sl_)
        act(winb[:, k, :], stg[sl_][:].bitcast(F32), AF.Copy, ['stg%d' % sl_, 'gm'], ['winb%d' % k], scale=gm[:, k:k + 1])
    for ci in range(2):
        for ct in range(4):
            tr(bank(ct % 2)[:, 0:128], Cn[:, ci, ct, :, :].rearrange("p a b -> p (a b)"), ident[:],
               ['Cn', 'ident'], ['psb%d' % (ct % 2)])
            for g2 in range(2):
                hs = slice(64 * g2, 64 * g2 + 64)
                src = AP(ps, ps[hs, ct % 2, g2 * 16:g2 * 16 + 1].offset, [[ps[:].ap[0][0], 64], [32, 4], [1, 16]])
                if ci == 0:
                    R.op('dve', lambda e, s=src, hs=hs, ct=ct: e.tensor_copy(out=CT[hs, 0, 4 * ct:4 * ct + 4, :], in_=s),
                         ['psb%d' % (ct % 2)], ['CT'])
                else:
                    R.op('dve', lambda e, s=src, hs=hs, ct=ct: e.tensor_copy(out=CTim[hs, 4 * ct:4 * ct + 4, :], in_=s),
                         ['psb%d' % (ct % 2)], ['CTim'])
    ts('dve', CT[:, 1, :, :], CTim[:], -1.0, None, ALU.mult, None, ['CTim'], ['CT'])

    dt_ = sb("dt_", [128, 16], F32, p0); aa = sb("aa", [128, 16], F32, p0); th = sb("th", [128, 16], F32, p0)
    tA = sb("tA", [128, 16], F32, p0); tB = sb("tB", [128, 16], F32, p0)
    cs1 = sb("cs1", [128, 16], F32, p0); sn1 = sb("sn1", [128, 16], F32, p0)
    act(dt_[:], ldt[:], AF.Exp, ['ldt'], ['dt_'])
    tt('dve', aa[:], lre[:], dt_[:], ALU.mult, ['lre', 'dt_'], ['aa'])
    tt('dve', th[:], lim[:], dt_[:], ALU.mult, ['lim', 'dt_'], ['th'])

    act(sn1[:], th[:], AF.Sin, ['th'], ['sn1'], scale=1.0 / 16)
    ts('dve', tA[:], th[:], 1.0 / 16, PI / 2, ALU.mult, ALU.add, ['th'], ['tA'])
    act(cs1[:], tA[:], AF.Sin, ['tA'], ['cs1'])
    for _ in range(4):
        tt('dve', tA[:], cs1[:], cs1[:], ALU.mult, ['cs1'], ['tA'])
        tt('dve', tB[:], sn1[:], sn1[:], ALU.mult, ['sn1'], ['tB'])
        tt('dve', sn1[:], sn1[:], cs1[:], ALU.mult, ['sn1', 'cs1'], ['sn1'])
        ts('dve', sn1[:], sn1[:], 2.0, None, ALU.mult, None, ['sn1'], ['sn1'])
        tt('dve', cs1[:], tA[:], tB[:], ALU.subtract, ['tA', 'tB'], ['cs1'])

    ut_c = sb("ut_c", [128, 17, 16], F32, p0); ut_s = sb("ut_s", [128, 17, 16], F32, p0)
    tmp1 = sb("tmp1", [128, 1024], F32, p0t); tmp2 = sb("tmp2", [128, 1024], F32, p0t)

    def powtab_steps(tc, tsn, bc, bs, N, key, rk, tmp1, tmp2):
        steps = []

        def init():
            R.op('dve', lambda e: e.memset(tc[:, 0, :], 1.0), [], [key])
            R.op('dve', lambda e: e.memset(tsn[:, 0, :], 0.0), [], [key])
            R.op('dve', lambda e: e.tensor_copy(out=tc[:, 1, :], in_=bc), rk, [key])
            R.op('dve', lambda e: e.tensor_copy(out=tsn[:, 1, :], in_=bs), rk, [key])
        steps.append(init)
        k = 1
        while k < N:
            n = min(k, N - k)

            def step_re(k=k, n=n):
                bcst = lambda t: AP(t, t[:, k, :].offset, [[t[:].ap[0][0], 128], [0, n], [1, 16]])
                t1 = tmp1[:, 0:n * 16].rearrange("p (a b) -> p a b", b=16)
                t2 = tmp2[:, 0:n * 16].rearrange("p (a b) -> p a b", b=16)
                tt('dve', t1, tc[:, 1:1 + n, :], bcst(tc), ALU.mult, [key], ['tmp1'])
                tt('dve', t2, tsn[:, 1:1 + n, :], bcst(tsn), ALU.mult, [key], ['tmp2'])
                tt('dve', tc[:, k + 1:k + 1 + n, :], t1, t2, ALU.subtract, ['tmp1', 'tmp2'], [key])

            def step_im(k=k, n=n):
                bcst = lambda t: AP(t, t[:, k, :].offset, [[t[:].ap[0][0], 128], [0, n], [1, 16]])
                t1 = tmp1[:, 0:n * 16].rearrange("p (a b) -> p a b", b=16)
                t2 = tmp2[:, 0:n * 16].rearrange("p (a b) -> p a b", b=16)
                tt('dve', t1, tsn[:, 1:1 + n, :], bcst(tc), ALU.mult, [key], ['tmp1'])
                tt('dve', t2, tc[:, 1:1 + n, :], bcst(tsn), ALU.mult, [key], ['tmp2'])
                tt('dve', tsn[:, k + 1:k + 1 + n, :], t1, t2, ALU.add, ['tmp1', 'tmp2'], [key])
            steps.append(step_re)
            steps.append(step_im)
            k += n
        return steps

    def powtab(*a):
        for st_ in powtab_steps(*a):
            st_()
    powtab(ut_c, ut_s, cs1[:], sn1[:], 16, 'ut', ['cs1', 'sn1'], tmp1, tmp2)
    R.op('dve', lambda e: e.tensor_copy(out=u16c[:], in_=ut_c[:, 16, :]), ['ut'], ['u16'])
    R.op('dve', lambda e: e.tensor_copy(out=u16s[:], in_=ut_s[:, 16, :]), ['ut'], ['u16'])

    Mg = sb("Mg", [128, 17, 16], F32, p0)
    for n in range(17):
        act(Mg[:, n, :], aa[:], AF.Exp, ['aa'], ['Mg'], scale=float(n))
    tt('dve', Pre[:], Mg[:], ut_c[:], ALU.mult, ['Mg', 'ut'], ['P'])
    tt('dve', Pim[:], Mg[:], ut_s[:], ALU.mult, ['Mg', 'ut'], ['P'])
    act(r16[:], aa[:], AF.Exp, ['aa'], ['r16'], scale=16.0)

    fre = sb("fre", [128, 16], F32, p0); fim = sb("fim", [128, 16], F32, p0)
    nr = sb("nr", [128, 16], F32, p0); den = sb("den", [128, 16], F32, p0)
    ts('dve', nr[:], Pre[:, 1, :], -1.0, None, ALU.add, None, ['P'], ['nr'])
    tt('dve', den[:], lre[:], lre[:], ALU.mult, ['lre'], ['den'])
    tt('dve', tA[:], lim[:], lim[:], ALU.mult, ['lim'], ['tA'])
    tt('dve', den[:], den[:], tA[:], ALU.add, ['den', 'tA'], ['den'])
    R.op('dve', lambda e: e.reciprocal(out=den[:], in_=den[:]), ['den'], ['den'])
    tt('dve', tA[:], nr[:], lre[:], ALU.mult, ['nr', 'lre'], ['tA'])
    tt('dve', tB[:], Pim[:, 1, :], lim[:], ALU.mult, ['P', 'lim'], ['tB'])
    tt('dve', tA[:], tA[:], tB[:], ALU.add, ['tA', 'tB'], ['tA'])
    tt('dve', fre[:], tA[:], den[:], ALU.mult, ['tA', 'den'], ['fre'])
    tt('dve', tA[:], Pim[:, 1, :], lre[:], ALU.mult, ['P', 'lre'], ['tA'])
    tt('dve', tB[:], nr[:], lim[:], ALU.mult, ['nr', 'lim'], ['tB'])
    tt('dve', tA[:], tA[:], tB[:], ALU.subtract, ['tA', 'tB'], ['tA'])
    tt('dve', fim[:], tA[:], den[:], ALU.mult, ['tA', 'den'], ['fim'])

    def b16(t, n=None):
        return AP(t, t[:].offset, [[t[:].ap[0][0], 128], [1, 16], [0, 16]])
    t1v = tmp1[:, 0:256].rearrange("p (a b) -> p a b", b=16)
    t2v = tmp2[:, 0:256].rearrange("p (a b) -> p a b", b=16)
    cmul('dve', Bb[:, 0, :, :], Bb[:, 1, :, :], Bsp[:, 0, :, :], Bsp[:, 1, :, :], b16(fre), b16(fim), t1v, t2v,
         ['Bsp', 'fre', 'fim'], ['Bb'], 'tmp')

    prow = Pre[:].ap[0][0]

    def gen_lag(scope, do_fir, wfir=None, extra_ops=None):
        CTd = sb("CTd", [128, 2, 16, 2, 16], F32, scope)
        E = [sb("E%d" % i, [128, 2, 16, 2, 16], F32, scope) for i in range(3)]
        g1 = sb("g1", [128, 16, 16], F32, scope); g2t = sb("g2t", [128, 16, 16], F32, scope)
        g3 = sb("g3", [128, 128], F32, scope)
        Gtc = [sb("Gt%d" % c_, [128, 16, 16], F32, scope) for c_ in range(2)]
        Eb = [sb("Eb%d" % i, [128, 2, 16, 2, 16], BF16, scope) for i in range(2)]
        identb0 = sb("identb0", [128, 128], BF16, scope)
        R.op('dve', lambda e: e.tensor_copy(out=identb0[:], in_=ident[:]), ['ident'], ['identb0'])
        ptw = [ps[:, 4 + a_, :].bitcast(BF16) for a_ in range(4)]
        if True:
            for g2 in range(2):
                R.op('dve', lambda e, g2=g2: e.tensor_copy(out=CTd[:, :, :, g2, :], in_=CT[:]), ['CT'], ['CTd'])
        for i in range(3):
            R.op('pool', lambda e, i=i: e.memset(E[i][:], 0.0), [], ['E%d' % i])
        def gen_E(tau):
            Ei = E[tau % 3]; ek = 'E%d' % (tau % 3)
            pb = lambda t: AP(t, t[:, tau, :].offset, [[prow, 128], [1, 16], [0, 16]])
            cmul('dve', Gtc[0][:], Gtc[1][:], Bb[:, 0, :, :], Bb[:, 1, :, :], pb(Pre), pb(Pim),
                 g1[:], g2t[:], ['Bb', 'P'], ['Gt'], 'gt')
            for g2 in range(2):
                hs = slice(64 * g2, 64 * g2 + 64)
                for c_ in range(2):
                    R.op('act', lambda e, hs=hs, g2=g2, Ei=Ei, c_=c_: e.copy(out=Ei[hs, c_, :, g2, :], in_=Gtc[c_][hs, :, :]), ['Gt'], [ek])
        gen_E(0)
        for tau in range(16):
            Ei = E[tau % 3]; ek = 'E%d' % (tau % 3)
            if tau + 1 < 16:
                gen_E(tau + 1)
            if extra_ops:
                extra_ops.pop(0)()
            for ct in range(4):
                if True:
                    bk = 2 + (ct % 2)
                    for c in range(2):
                        mm(bank(bk)[:, 0:128], Ei[:, c, 4 * ct:4 * ct + 4, :, :].rearrange("p a b c -> p (a b c)"),
                           CTd[:, c, 4 * ct:4 * ct + 4, :, :].rearrange("p a b c -> p (a b c)"), c == 0, c == 1,
                           [ek, 'CTd'], ['psb%d' % bk])
                    if tau == 0:
                        tt('dve', g3[:], bank(bk)[:, 0:128], bdm[:], ALU.mult, ['psb%d' % bk, 'bdm'], ['g3'])
                        R.op('dve', lambda e, ct=ct: e.scalar_tensor_tensor(out=wfir[:, 0, ct, :], in0=ident[:], scalar=dk[:, ct:ct + 1],
                                                                           in1=g3[:], op0=ALU.mult, op1=ALU.add),
                             ['g3', 'ident', 'dk'], ['wfir'])
                    else:
                        tt('dve', wfir[:, tau, ct, :], bank(bk)[:, 0:128], bdm[:], ALU.mult, ['psb%d' % bk, 'bdm'], ['wfir'])
                if True:
                    if ct == 0:
                        R.op('act', lambda e, Ei=Ei, tau=tau: e.copy(out=Eb[tau % 2][:], in_=Ei[:]), [ek], ['Eb%d' % (tau % 2)])
                    for c in range(2):
                        bk2 = 4 + ((ct * 2 + c) % 4)
                        R.op('pe', lambda e, bk2=bk2, ct=ct, c=c, tau=tau: e.transpose(
                            ptw[bk2 - 4][:, 0:128], Eb[tau % 2][:, c, 4 * ct:4 * ct + 4, :, :].rearrange("p a b c -> p (a b c)"), identb0[:]),
                            ['Eb%d' % (tau % 2), 'identb0'], ['psb%d' % bk2])
                        R.op('act', lambda e, bk2=bk2, ct=ct, c=c, j=15 - tau: e.copy(out=wz[:, ct, j, c, :], in_=ptw[bk2 - 4][:, 0:128]),
                             ['psb%d' % bk2], ['wz'])
    R.emit("p0a")
    p0t.close()
    wstg.close()
    wfir = sb("wfir", [128, 16, 4, 128], BF16, p0)
    wy = sb("wy", [128, 2, 16, 256], BF16, p0)
    def pbi(t, ih):
        return AP(t, t[:, 1 + 8 * ih, :].offset, [[prow, 128], [1, 16], [16, 8], [0, 16]])

    def cb(t, c=None):
        base = t[:, c, :, :] if c is not None else t[:]
        return AP(t, base.offset, [[base.ap[0][0], 128], [16, 16], [0, 8], [1, 16]])
    big1 = sb("big1", [128, 2048], F32, p0); big2 = sb("big2", [128, 2048], F32, p0)
    b1v = big1[:].rearrange("p (a b c) -> p a b c", a=16, b=8)
    b2v = big2[:].rearrange("p (a b c) -> p a b c", a=16, b=8)
    wy_ops = []
    for ih in range(2):
        w0 = wy[:, 0, :, :].rearrange("p a (b c) -> p a b c", b=16)[:, :, 8 * ih:8 * ih + 8, :]
        w1 = wy[:, 1, :, :].rearrange("p a (b c) -> p a b c", b=16)[:, :, 8 * ih:8 * ih + 8, :]
        wy_ops.append(lambda ih=ih: tt('dve', b1v, cb(CT, 0), pbi(Pre, ih), ALU.mult, ['CT', 'P'], ['big1']))
        wy_ops.append(lambda ih=ih: tt('dve', b2v, cb(CTim), pbi(Pim, ih), ALU.mult, ['CTim', 'P'], ['big2']))
        wy_ops.append(lambda w0=w0: tt('dve', w0, b1v, b2v, ALU.subtract, ['big1', 'big2'], ['wy']))
        wy_ops.append(lambda ih=ih: tt('dve', b1v, cb(CT, 0), pbi(Pim, ih), ALU.mult, ['CT', 'P'], ['big1']))
        wy_ops.append(lambda ih=ih: tt('dve', b2v, cb(CTim), pbi(Pre, ih), ALU.mult, ['CTim', 'P'], ['big2']))
        wy_ops.append(lambda: tt('dve', b1v, b1v, b2v, ALU.add, ['big1', 'big2'], ['big1']))
        wy_ops.append(lambda w1=w1: ts('dve', w1, b1v, -1.0, None, ALU.mult, None, ['big1'], ['wy']))
    gen_lag(p0, True, wfir, wy_ops)
    dma('pool', relE.ap()[:, 511:767], relb.ap()[:, 0:256], [], ['relEa'], '@relE')
    dma('pool', AP(relE, 0, [[768, 8], [1, 511], [1, 1]]), AP(relb, 0, [[257, 8], [0, 511], [1, 1]]), [], ['relEb'], '@relE')
    dma('pool', relE.ap()[:, 767:768], relb.ap()[:, 0:1], [], ['relEc'], '@relE')
    R.seal('relE')
    for hd in range(8):
        dma('pool', AP(relE2, hd * 98304, [[768, 128], [1, 768]]), AP(relE, hd * 768, [[0, 128], [1, 768]]),
            ['relEa', 'relEb', 'relEc'], ['relS%d' % hd], '@relS')
    R.seal('relS')
    while wy_ops:
        wy_ops.pop(0)()
    dma('sp', wsp.ap()[:, 0:8192], wfir[:].rearrange("p a b c -> p (a b c)"), ['wfir'], ['wsp'], '@wsp')
    dma('sp', wsp.ap()[:, 8192:16384], wy[:].rearrange("p a b c -> p (a b c)"), ['wy'], ['wsp'], '@wsp')
    R.seal('wsp')
    R.op('sp', None, ['wsp'], [])
    R.emit("p0")
    p0.close()

    ucb = sb("ucb", [128, 129, 16], F32, wst); usb = sb("usb", [128, 129, 16], F32, wst)
    ust = Scope()
    sprev = sb("sprev", [128, 2, 16, 128], BF16, ust)
    uT = sb("uT", [128, 4, TOK], BF16, ust)
    p2 = Scope()
    qT = sb("qT", [128, 4, TOK], BF16, p2)
    kT = sb("kT", [128, 4, TOK + 512], BF16, p2)
    Vt = sb("Vt", [128, 20, 512], BF16, p2)
    p1 = Scope()
    NXT = 3
    NDG = 3
    NPG = 4
    xt = [sb("xt%d" % i, [128, D], F32, p1) for i in range(NXT)]
    xb = [sb("xb%d" % i, [128, D], BF16, p1) for i in range(2)]
    identb1 = sb("identb1", [128, 128], BF16, p1)
    hT = sb("hT", [128, 8, 512], BF16, p1)
    ssq = sb("ssq", [128, NDG], F32, p1)
    Zbs = [sb("Zb%d" % i, [128, 2, NPG, 128], F32, p1) for i in range(2)]
    Zm = sb("Zm", [128, 2, NPG, 128], F32, p1)
    ztj = sb("ztj", [128, 2 * NPG * 128], F32, p1)
    zt1 = ztj[:, 0:NPG * 128].rearrange("p (a b) -> p a b", a=NPG)
    zt2 = ztj[:, NPG * 128:2 * NPG * 128].rearrange("p (a b) -> p a b", a=NPG)
    junk = ztj
    D0g = sb("D0g", [128, NPG, 128], F32, p1)
    cf = sb("cf", [128, 2, NPG], F32, p1)
    R.op('dve', lambda e: e.tensor_copy(out=identb1[:], in_=ident[:]), ['ident'], ['identb1'])
    ub_steps = powtab_steps(ucb, usb, u16c[:], u16s[:], 128, 'ub', ['u16'],
                            Zbs[0][:].rearrange("p a b c -> p (a b c)"), Zbs[1][:].rearrange("p a b c -> p (a b c)"))
    ptx = [ps[:, a_, :].bitcast(BF16) for a_ in range(4)]

    def tabv(t, lo, n, pg):
        return AP(t, t[:, lo, NPG * pg:NPG * pg + 1].offset, [[t[:].ap[0][0], 128], [1, NPG], [16, n]])


    def stageL(gt):
        xs = gt % NXT
        dma('sp', xt[xs][:], xw.ap()[gt * 128:(gt + 1) * 128, :], [], ['xt%d' % xs], 'xt%d' % xs)

    def stageA(gt):
        xs = gt % NXT
        s_ = gt % NDG
        b_ = gt % 2
        xk = 'xt%d' % xs; sk = 'ssq%d' % s_
        R.op('dve', lambda e, xs=xs, s_=s_: e.scalar_tensor_tensor(out=junk[:], in0=xt[xs][:], scalar=1.0, in1=xt[xs][:],
                                                                   op0=ALU.mult, op1=ALU.mult, accum_out=ssq[:, s_:s_ + 1]),
             [xk], ['zt1', 'zt2', sk])
        act(ssq[:, s_:s_ + 1], ssq[:, s_:s_ + 1], AF.Sqrt, [sk, 'epsc'], [sk], scale=1.0 / D, bias=epsc[:])
        R.op('dve', lambda e, o_=ssq[:, s_:s_ + 1]: e.reciprocal(out=o_, in_=o_), [sk], [sk])
        act(xb[b_][:], xt[xs][:], AF.Copy, [xk, sk], ['xb%d' % b_], scale=ssq[:, s_:s_ + 1])

    def stageB(gt, tl, ch):
        b_ = gt % 2
        bk = gt % 4
        for k in range(8):
            R.op('pe', lambda e, k=k, bk=bk, b_=b_: e.transpose(ptx[bk][:, k * 128:(k + 1) * 128], xb[b_][:, k * 128:(k + 1) * 128], identb1[:]),
                 ['xb%d' % b_, 'identb1'], ['psb%d' % bk], inc=(k == 7))
        src = ptx[bk].rearrange("p (a b) -> p a b", a=8)
        dst = hTb[ch % 2][:, :, tl * 128:(tl + 1) * 128]
        R.op('dve', lambda e, src=src, dst=dst: e.tensor_copy(out=dst, in_=src), ['psb%d' % bk], ['hT%d' % (ch % 2)])

    hTb = [hT, sprev[:].rearrange("p a b c -> p (a b c)").rearrange("p (a b) -> p a b", a=8)]

    def projpart(ch, part):
        hb_ = hTb[ch % 2]; hk = 'hT%d' % (ch % 2)
        wcol = ch % 4
        ct = part
        bk = 4 + ct % 2
        for k in range(8):
            mm(bank(bk), winb[:, k, ct * 128:(ct + 1) * 128], hb_[:, k, :], k == 0, k == 7, ['winb%d' % k, hk], ['psb%d' % bk])
        R.op('act', (lambda e, ct=ct, bk=bk, wcol=wcol: e.copy(
            out=uT[:, ct, wcol * 512:(wcol + 1) * 512], in_=bank(bk))), ['psb%d' % bk], ['uT'])
        if ch >= 12:
            oc = ch - 12
            bk = 6
            for k in range(8):
                mm(bank(bk), winb[:, k, 512 + ct * 128:512 + (ct + 1) * 128], hb_[:, k, :], k == 0, k == 7,
                   ['winb%d' % k, hk], ['psb%d' % bk])
            R.op('dve', (lambda e, ct=ct, bk=bk, oc=oc: e.tensor_copy(
                out=qT[:, ct, oc * 512:(oc + 1) * 512], in_=bank(bk))), ['psb%d' % bk], ['qT'])
        if ch >= 11:
            kc = ch - 11
            bk = 7
            for k in range(8):
                mm(bank(bk), winb[:, k, 1024 + ct * 128:1024 + (ct + 1) * 128], hb_[:, k, :], k == 0, k == 7,
                   ['winb%d' % k, hk], ['psb%d' % bk])
            R.op('act', (lambda e, ct=ct, bk=bk, kc=kc: e.copy(
                out=kT[:, ct, kc * 512:(kc + 1) * 512], in_=bank(bk))), ['psb%d' % bk], ['kT'])
            tl = part
            bk = 6 if ch == 11 else 5
            for k in range(8):
                mm(bank(bk), hb_[:, k, tl * 128:(tl + 1) * 128], winb[:, k, 1536:2048], k == 0, k == 7,
                   ['winb%d' % k, hk], ['psb%d' % bk])
            R.op('dve', (lambda e, tl=tl, bk=bk, kc=kc: e.tensor_copy(
                out=Vt[:, kc * 4 + tl, :], in_=bank(bk))), ['psb%d' % bk], ['Vt'])

    def zscan(ch, own):
        for pg in range(16 // NPG):
            psl = slice(NPG * pg, NPG * pg + NPG)
            Zb = Zbs[pg % 2]; zk = 'Zb%d' % (pg % 2)
            for j in range(16):
                for pl in range(NPG):
                    pair = NPG * pg + pl
                    ct, kk = pair // 4, pair % 4
                    for c in range(2):
                        bk = 2 * pl + c
                        rhs = AP(uT, uT[32 * kk:32 * kk + 32, ct, j:j + 1].offset, [[uT[:].ap[0][0], 32], [16, 128]])
                        mm(bank(bk)[:, 0:128], wz[32 * kk:32 * kk + 32, ct, j, c, :], rhs, j == 0, j == 15,
                           ['wz', 'uT'], ['psb%d' % bk], tp=(32 * kk, 0))
            for pl in range(NPG):
                for c in range(2):
                    bk = 2 * pl + c
                    R.op('act', (lambda e, pl=pl, c=c, bk=bk, Zb=Zb: e.copy(out=Zb[:, c, pl, :], in_=bank(bk)[:, 0:128])),
                         ['psb%d' % bk], [zk])
            cb_, sb_ = tabv(ucb, 0, 128, pg), tabv(usb, 0, 128, pg)
            tt('dve', zt1, Zb[:, 1, :, :], sb_, ALU.mult, [zk, 'ub'], ['zt1'])
            tt('dve', Zm[:, 0, :, :], Zb[:, 0, :, :], cb_, ALU.mult, [zk, 'ub'], ['Zm0'])
            tt('dve', Zm[:, 0, :, :], Zm[:, 0, :, :], zt1, ALU.add, ['Zm0', 'zt1'], ['Zm0'])
            tt('pool', zt2, Zb[:, 0, :, :], sb_, ALU.mult, [zk, 'ub'], ['zt2'])
            tt('pool', Zm[:, 1, :, :], Zb[:, 1, :, :], cb_, ALU.mult, [zk, 'ub'], ['Zm1'])
            tt('pool', Zm[:, 1, :, :], Zm[:, 1, :, :], zt2, ALU.subtract, ['Zm1', 'zt2'], ['Zm1'])
            r_b = AP(r16, r16[:, NPG * pg:NPG * pg + 1].offset, [[r16[:].ap[0][0], 128], [1, NPG], [0, 128]])
            R.op('dve', lambda e, r_b=r_b: e.tensor_copy(out=D0g[:], in_=r_b), ['r16'], ['D0g'])
            R.op('dve', lambda e: e.memset(D0g[:, :, 0], 0.0), [], ['D0g'])
            r_c = AP(r16, r16[:, NPG * pg:NPG * pg + 1].offset, [[r16[:].ap[0][0], 128], [0, 2], [1, NPG]])
            tt('dve', cf[:], carry[:, :, psl], r_c, ALU.mult, ['carry', 'r16'], ['cf'])
            for c in range(2):
                tt('dve', Zm[:, c, :, 0], Zm[:, c, :, 0], cf[:, c, :], ALU.add, ['Zm%d' % c, 'cf'], ['Zm%d' % c])
                R.op('dve', lambda e, c=c, Zb=Zb: e.tensor_tensor_scan(
                    out=Zb[:, c, :, :].rearrange("p a b -> p (a b)"), data0=D0g[:].rearrange("p a b -> p (a b)"),
                    data1=Zm[:, c, :, :].rearrange("p a b -> p (a b)"), initial=0.0,
                    op0=ALU.mult, op1=ALU.add), ['Zm%d' % c, 'D0g'], [zk])
            if own:
                R.op('act', lambda e, psl=psl: e.copy(out=sprev[:, :, psl, 0], in_=send[:, :, psl]), ['send'], ['sprev', 'hT1'])
                cmul('dve', Zm[:, 0, :, 1:128], Zm[:, 1, :, 1:128], Zb[:, 0, :, 0:127], Zb[:, 1, :, 0:127],
                     tabv(ucb, 0, 127, pg), tabv(usb, 0, 127, pg), zt1[:, :, 0:127], zt2[:, :, 0:127], [zk, 'ub'], ['Zm0', 'Zm1'], 'zt')
                R.op('act', lambda e, psl=psl: e.copy(out=sprev[:, :, psl, 1:128], in_=Zm[:, :, :, 1:128]), ['Zm0', 'Zm1'], ['sprev', 'hT1'])
            else:
                l1 = zt1[:, :, 0]; l2 = zt2[:, :, 0]
                if ch == 11:
                    cmul('dve', send[:, 0, psl], send[:, 1, psl], Zb[:, 0, :, 127], Zb[:, 1, :, 127], ucb[:, 127, psl], usb[:, 127, psl],
                         l1, l2, [zk, 'ub'], ['send'], 'zt')
                cmul('dve', carry[:, 0, psl], carry[:, 1, psl], Zb[:, 0, :, 127], Zb[:, 1, :, 127], ucb[:, 128, psl], usb[:, 128, psl],
                     l1, l2, [zk, 'ub'], ['carry'], 'zt')

    for g_ in range(2):
        stageL(g_)
    stageA(0)
    for ch in range(16):
        for tl in range(4):
            gt = ch * 4 + tl
            if gt + 2 < 64:
                stageL(gt + 2)
            if gt + 1 < 64:
                stageA(gt + 1)
            stageB(gt, tl, ch)
            if ub_steps:
                ub_steps.pop(0)()
            if ch >= 1:
                projpart(ch - 1, tl)
        if ch >= 1 and (ch - 1) % 4 == 3:
            zscan(ch - 1, ch - 1 >= 12)
    for part in range(4):
        projpart(15, part)
    zscan(15, True)
    R.emit("p1")
    p1.close()
    wst.close()

    mixT = sb("mixT", [128, 8, TOK], BF16)
    w4 = Scope()
    wfir = sb("wfir", [128, 16, 4, 128], BF16, w4)
    p3 = Scope()
    Tb = sb("Tb", [128, 8, 640], F32, p3)
    b8 = sb("b8", [128, 640], F32, p3)
    Thi = sb("Thi", [128, 8, 640], BF16, p3)
    hbt = sb("hbt", [128, 512], F32, p3)
    hb8 = sb("hb8", [128, 512], BF16, p3)
    bandt = sb("bandt", [128, 640], F32, p3)
    qzt = [sb("qzt%d" % i, [128, 2, 4, 128], BF16, p3) for i in range(2)]
    Pf = [sb("Pf%d" % i, [128, 640], BF16, p3) for i in range(2)]
    PT = [sb("PT%d" % i, [128, 5, 128], BF16, p3) for i in range(2)]
    identb = sb("identb", [128, 128], BF16, p3)
    Oat = [sb("Oat%d" % i, [128, 512], F32, p3) for i in range(2)]
    On = [sb("On%d" % i, [128, 512], F32, p3) for i in range(2)]
    stat = sb("stat", [128, 8, 4], F32, p3)
    ast = sb("ast", [128, 2], F32, p3)
    R.op('dve', lambda e: e.tensor_copy(out=identb[:], in_=ident[:]), ['ident'], ['identb'])
    for b_ in range(2):
        R.op('pool', lambda e, b_=b_: e.memset(qzt[b_][:], 0.0), [], ['qzt%d' % b_])
    dma('sp', hbt[:], c_hb.ap(), [], ['hbt'], 'hbt')
    dma('sp', bandt[:], c_band.ap(), [], ['bandt'], 'bandt')
    ts('dve', hb8[:], hbt[:], 1.0, None, ALU.mult, None, ['hbt'], ['hb8'])
    for hd in range(8):
        dma('sp', Tb[:, hd, :], AP(relE2, hd * 98304 + 127, [[767, 128], [1, 640]]), ['relS%d' % hd], ['Tb%d' % hd, 'TbL%d' % hd], '@Tb')
    R.seal('Tb')
    dma('sp', wfir[:].rearrange("p a b c -> p (a b c)"), wsp.ap()[:, 0:8192], ['wsp'] + ['TbL%d' % q_ for q_ in range(8)], ['wfir'], 'wfirL')
    def tprep(hd):
        R.op('dve', lambda e, hd=hd: e.scalar_tensor_tensor(out=Tb[:, hd, :], in0=Tb[:, hd, :], scalar=1.0, in1=b8[:], op0=ALU.mult, op1=ALU.add),
             ['Tb%d' % hd, 'b8'], ['Tb%d' % hd])
        R.op('dve', lambda e, hd=hd: e.tensor_copy(out=Thi[:, hd, :], in_=Tb[:, hd, :]), ['Tb%d' % hd], ['Thi%d' % hd])
    ts('dve', b8[:], bandt[:], 1.0, None, ALU.mult, None, ['bandt'], ['b8'])
    for ct_ in range(4):
        ts('dve', qT[:, ct_, :], qT[:, ct_, :], 0.125, None, ALU.mult, None, ['qT'], ['qT'])
    def issue_ffn_casts():
        for (src, dst, key) in ((w_gate, wg_s, 'wg_s'), (w_up, wu_s, 'wu_s')):
            for h in range(2):
                dma('pool', AP(dst, h * 512 * DFF, [[DFF, 512], [1408, 2], [1, 1408]]),
                    AP(src, h * 512 * DFF, [[DFF, 512], [1408, 2], [1, 1408]]), ['TbL%d' % q_ for q_ in range(8)], [key + str(h)], '@ffnw')
        for h in range(2):
            dma('pool', wd_s.ap()[h * 1408:(h + 1) * 1408, :], w_down.ap()[h * 1408:(h + 1) * 1408, :], ['TbL%d' % q_ for q_ in range(8)],
                ['wd_s%d' % h], '@ffnw')
        R.seal('ffnw')
    psrow = ps[:].ap[0][0]
    ptb = [ps[:, 4 + a_, :].bitcast(BF16) for a_ in range(2)]

    def S1(n):
        i, hd = n // 8, n % 8
        tq, h2 = hd // 2, hd % 2
        a_ = n % 2
        sbk = 'pss%d' % a_
        qb = i % 2
        if n < 8:
            tprep(hd)
        if hd == 0:
            for i2 in ([0, 1] if i == 0 else ([i + 1] if i + 1 < 16 else [])):
                for hh in range(2):
                    hs = slice(64 * hh, 64 * hh + 64)
                    R.op('act' if hh else 'pool', (lambda e, hh=hh, hs=hs, i2=i2: (e.copy if hasattr(e, 'copy') else e.tensor_copy)(
                        out=qzt[i2 % 2][hs, hh, :, :], in_=qT[hs, :, 128 * i2:128 * i2 + 128])), ['qT'], ['qzt%d' % (i2 % 2)])
        for (c0_, nn, bk) in ((0, 512, 2 * a_), (512, 128, 2 * a_ + 1)):
            o_ = bank(bk)[:, 0:nn]
            mm(o_, qzt[qb][:, h2, tq, :], kT[:, tq, 128 * i + c0_:128 * i + c0_ + nn], True, False,
               ['qzt%d' % qb, 'kT'], [sbk], inc=False)
            last = not (i < 4 and c0_ == 0)
            mm(o_, identb[:], Thi[:, hd, c0_:c0_ + nn], False, last, ['identb', 'Thi%d' % hd], [sbk], inc=(c0_ == 512))
            if not last:
                w_ = 512 - 128 * i
                mm(bank(bk)[:, 0:w_], identb[:], hb8[:, 128 * i:512], False, True, ['identb', 'hb8'], [sbk], inc=False)
        sv = AP(ps, ps[:, 2 * a_, 0:1].offset, [[psrow, 128], [1, 640]])
        R.op('dve', lambda e, sv=sv, hd=hd: e.reduce_max(out=stat[:, hd, 1:2], in_=sv, axis=AX.X, negate=True), [sbk], ['st%d' % hd])

    def S2a(n):
        i, hd = n // 8, n % 8
        a_ = n % 2
        sv = AP(ps, ps[:, 2 * a_, 0:1].offset, [[psrow, 128], [1, 640]])
        act(Pf[a_][:], sv, AF.Exp, ['pss%d' % a_, 'st%d' % hd], ['Pf%d' % a_, 'sr%d' % hd], scale=1.0, bias=stat[:, hd, 1:2],
            accum=stat[:, hd, 2:3])
        R.op('dve', lambda e, hd=hd: e.reciprocal(out=stat[:, hd, 3:4], in_=stat[:, hd, 2:3]), ['sr%d' % hd], ['sr%d' % hd])

    def S2b(n):
        a_ = n % 2
        for kt in range(5):
            R.op('pe', lambda e, a_=a_, kt=kt: e.transpose(ptb[a_][:, kt * 128:(kt + 1) * 128], Pf[a_][:, kt * 128:(kt + 1) * 128], identb[:]),
                 ['Pf%d' % a_, 'identb'], ['pst%d' % a_], inc=(kt == 4))
        R.op('act', lambda e, a_=a_: e.copy(out=PT[a_][:].rearrange("p a b -> p (a b)"), in_=ptb[a_][:, 0:640]),
             ['pst%d' % a_], ['PT%d' % a_])

    def S3(n):
        i, hd = n // 8, n % 8
        a_ = n % 2
        ob = i % 2
        for kt in range(5):
            mm(bank(6 + a_)[:, 0:64], PT[a_][:, kt, :], Vt[:, i + kt, hd * 64:(hd + 1) * 64], kt == 0, kt == 4,
               ['PT%d' % a_, 'Vt'], ['pso%d' % a_])
        ts('dve', Oat[ob][:, hd * 64:(hd + 1) * 64], bank(6 + a_)[:, 0:64], stat[:, hd, 3:4], None, ALU.mult, None,
           ['pso%d' % a_, 'sr%d' % hd], ['Oat%d' % ob])

    def S4(i):
        ob = i % 2
        ok = 'Oat%d' % ob; nk = 'On%d' % ob
        act(On[ob][:], Oat[ob][:], AF.Square, [ok], [nk, 'ast%d' % ob], accum=ast[:, ob:ob + 1])
        act(ast[:, ob:ob + 1], ast[:, ob:ob + 1], AF.Ln, ['ast%d' % ob, 'epsc'], ['ast%d' % ob], scale=1.0 / 512, bias=epsc[:])
        act(ast[:, ob:ob + 1], ast[:, ob:ob + 1], AF.Exp, ['ast%d' % ob], ['ast%d' % ob], scale=-0.5)
        ts('dve', On[ob][:], Oat[ob][:], ast[:, ob:ob + 1], None, ALU.mult, None, [ok, 'ast%d' % ob], [nk])
        for ct in range(4):
            tr(bank(7)[:, 128 + ct * 96:128 + ct * 96 + 96] if False else bank(7)[:, ct * 128:(ct + 1) * 128],
               On[ob][:, ct * 128:(ct + 1) * 128], ident[:], [nk, 'ident'], ['pso1'])
        for ct in range(4):
            ts('dve', mixT[:, 4 + ct, 128 * i:128 * i + 128], bank(7)[:, ct * 128:(ct + 1) * 128], ga[:, ct:ct + 1], None,
               ALU.mult, None, ['pso1', 'ga'], ['mixT'])

    NU = 128
    for n in range(NU + 8):
        if n == 8:
            issue_ffn_casts()
        if n < NU:
            S1(n)
        if 0 <= n - 1 < NU:
            S2a(n - 1)
        if 0 <= n - 2 < NU:
            S2b(n - 2)
        if 0 <= n - 3 < NU:
            S3(n - 3)
        m_ = n - 3 - 4
        if m_ >= 7 and m_ % 8 == 7 and m_ < NU:
            S4(m_ // 8)
    R.emit("p3")
    p3.close()
    p2.close()

    wy = sb("wy", [128, 2, 16, 256], BF16, w4)
    dma('sp', wy[:].rearrange("p a b c -> p (a b c)"), wsp.ap()[:, 8192:16384], ['wsp'], ['wy'], 'wyL')

    p4 = Scope()
    yT = sb("yT", [128, 4, TOK], F32, p4)
    p4a = Scope()
    Yb = sb("Yb", [128, 16, 8, 16], F32, p4a)
    urow = uT[:].ap[0][0]
    for chk in range(4):
        for ct in range(4):
            bk = (chk * 4 + ct) % 4
            for tau in range(16):
                if tau == 0:
                    o = bank(bk); rhs = uT[:, ct, chk * 512:(chk + 1) * 512]
                else:
                    o = AP(ps, ps[:, bk, tau:tau + 1].offset, [[ps[:].ap[0][0], 128], [16, 32], [1, 16 - tau]])
                    rhs = AP(uT, uT[:, ct, chk * 512:chk * 512 + 1].offset, [[urow, 128], [16, 32], [1, 16 - tau]])
                mm(o, wfir[:, tau, ct, :], rhs, tau == 0, tau == 15, ['wfir', 'uT'], ['psb%d' % bk])
            R.op('act', lambda e, ct=ct, chk=chk, bk=bk: e.copy(out=yT[:, ct, chk * 512:(chk + 1) * 512], in_=bank(bk)),
                 ['psb%d' % bk], ['yT%d' % q_ for q_ in range(8)])
    yrow = yT[:].ap[0][0]
    for ct in range(4):
        for gl in range(8):
            g = 8 * ct + gl
            pair, g2 = g // 2, g % 2
            bk = 4 + g % 4
            hs = slice(64 * g2, 64 * g2 + 64)
            for c in range(2):
                mm(bank(bk)[:, 0:256], sprev[hs, c, pair, :], wy[hs, c, pair, :], c == 0, c == 1, ['sprev', 'wy'], ['psb%d' % bk],
                   tp=(64 * g2, 0))
            R.op('act' if g % 2 else 'dve', (lambda e, gl=gl, bk=bk: (e.copy if hasattr(e, 'copy') else e.tensor_copy)(
                out=Yb[:, :, gl, :], in_=bank(bk)[:, 0:256].rearrange("p (a b) -> p a b", b=16))), ['psb%d' % bk], ['Yb'])
        for ib in range(4):
            bk = ib
            for q4 in range(4):
                i = 4 * ib + q4
                R.op('pe', lambda e, bk=bk, q4=q4, i=i: e.transpose(bank(bk)[:, q4 * 128:(q4 + 1) * 128],
                                                                   Yb[:, i, :, :].rearrange("p a b -> p (a b)"), ident[:]),
                     ['Yb', 'ident'], ['psb%d' % bk], inc=(q4 == 3))
            yv = AP(yT, yT[:, ct, 4 * ib:4 * ib + 1].offset, [[yrow, 128], [1, 4], [16, 128]])
            tt('dve', yv, yv, bank(bk).rearrange("p (a b) -> p a b", a=4), ALU.add,
               ['psb%d' % bk] + ['yT%d' % q_ for q_ in range(8)], ['yT%d' % q_ for q_ in range(8)])
    R.emit("p4a")
    p4a.close()
    w4.close()
    wob = sb("wob", [128, 8, D], BF16)
    wdb = sb("wdb", [128, NF, D], BF16)
    for k in range(8):
        dma('pool', wob[:, k, :], w_out.ap()[k * 128:(k + 1) * 128, :], [], ['wob'], '@wob')
    R.seal('wob')
    for h in range(2):
        dma('sp', wdb[:, 11 * h:11 * h + 11, :], AP(wd_s, h * 11 * 128 * D, [[D, 128], [128 * D, 11], [1, D]]),
            ['wd_s0', 'wd_s1'], ['wdb'], 'wdb%d' % h)
    ygb = sb("ygb", [128, 4, TOK], BF16, p4)
    tg = sb("tg", [128, 4, 256], F32, p4)
    sqb = [sb("sqb%d" % i, [128, 4, 256], BF16, p4) for i in range(2)]
    ssb = sb("ssb", [128, TOK], F32, p4)
    onesb = sb("onesb", [128, 128], BF16, p4)
    hbg = sb("hbg", [128, 4], F32, p4)
    lnh = sb("lnh", [128, 1], F32, p4)
    wgl = sb("wgl", [128, 4, 512], BF16, p4)
    for k in range(4):
        dma('pool', wgl[:, k, :], w_glu.ap()[k * 128:(k + 1) * 128, :], [], ['wgl'], '@wgl')
    R.seal('wgl')
    R.op('dve', lambda e: e.tensor_copy(out=onesb[:], in_=ones[:]), ['ones'], ['onesb'])
    R.op('dve', lambda e: e.memset(lnh[:], math.log(0.5)), [], ['lnh'])
    ts('dve', hbg[:], bg[:], 0.5, None, ALU.mult, None, ['bg'], ['hbg'])
    for chk in range(8):
        cs_ = slice(chk * 256, (chk + 1) * 256)
        for ct in range(4):
            act(yT[:, ct, cs_], yT[:, ct, cs_], AF.Gelu, ['yT%d' % chk], ['yT%d' % chk])
            R.op('dve', lambda e, ct=ct, cs_=cs_: e.tensor_copy(out=ygb[:, ct, cs_], in_=yT[:, ct, cs_]), ['yT%d' % chk], ['ygb%d' % chk])

    def Ymm(chk):
        cs_ = slice(chk * 256, (chk + 1) * 256)
        p_ = chk % 2
        for co in range(4):
            bk = 3 * p_ + co // 2
            for k in range(4):
                mm(bank(bk)[:, (co % 2) * 256:(co % 2) * 256 + 256], wgl[:, k, co * 128:(co + 1) * 128], ygb[:, k, cs_], k == 0, k == 3,
                   ['wgl', 'ygb%d' % chk], ['psb%d' % bk])

    def Yrest(chk):
        cs_ = slice(chk * 256, (chk + 1) * 256)
        b_ = chk % 2
        for co in range(4):
            bk = 3 * b_ + co // 2
            act(tg[:, co, :], bank(bk)[:, (co % 2) * 256:(co % 2) * 256 + 256], AF.Tanh, ['psb%d' % bk, 'hbg'], ['tg%d' % co], scale=0.5,
                bias=hbg[:, co:co + 1])
            R.op('dve', lambda e, co=co, cs_=cs_: e.scalar_tensor_tensor(out=yT[:, co, cs_], in0=tg[:, co, :], scalar=1.0, in1=yT[:, co, cs_],
                                                                       op0=ALU.add, op1=ALU.mult), ['tg%d' % co, 'yT%d' % chk], ['yT%d' % chk])
            tt('dve', sqb[b_][:, co, :], yT[:, co, cs_], yT[:, co, cs_], ALU.mult, ['yT%d' % chk], ['sq%d_%d' % (b_, co)])
        for co in range(4):
            mm(bank(3 * b_ + 2)[:, 0:256], onesb[:], sqb[b_][:, co, :], co == 0, co == 3, ['onesb', 'sq%d_%d' % (b_, co)], ['psb%d' % (3 * b_ + 2)])
        R.op('act', lambda e, b_=b_, cs_=cs_: e.copy(out=ssb[:, cs_], in_=bank(3 * b_ + 2)[:, 0:256]), ['psb%d' % (3 * b_ + 2)], ['ssb'])

    Ymm(0)
    for chk in range(8):
        if chk + 1 < 8:
            Ymm(chk + 1)
        Yrest(chk)
    act(ssb[:], ssb[:], AF.Ln, ['ssb', 'epsc'], ['ssb'], scale=1.0 / 2048, bias=epsc[:])
    act(ssb[:], ssb[:], AF.Exp, ['ssb', 'lnh'], ['ssb'], scale=-0.5, bias=lnh[:])
    for chk in range(4):
        cs_ = slice(chk * 512, (chk + 1) * 512)
        for co in range(4):
            R.op('dve', lambda e, co=co, cs_=cs_: e.scalar_tensor_tensor(out=mixT[:, co, cs_], in0=yT[:, co, cs_], scalar=gs[:, co:co + 1],
                                                                       in1=ssb[:, cs_], op0=ALU.mult, op1=ALU.mult),
                 ['yT%d' % (2 * chk), 'yT%d' % (2 * chk + 1), 'ssb', 'gs'], ['mixT'])
    R.emit("p4b")
    p4.close()
    ust.close()

    p5 = Scope()
    wgr = [sb("wgr%d" % i, [128, 8, 256], BF16, p5) for i in range(3)]
    wur = [sb("wur%d" % i, [128, 8, 256], BF16, p5) for i in range(3)]
    x1 = sb("x1", [128, 4, D], F32, p5)
    xr = [sb("xr%d" % i, [128, D], F32, p5) for i in range(2)]
    gffb = sb("gffb", [128, D], F32, p5)
    h2T = sb("h2T", [128, 8, 512], BF16, p5)
    actT = sb("actT", [128, NF, 512], BF16, p5)
    sl = [sb("sl%d" % i, [128, 512], F32, p5) for i in range(2)]
    gfb = sb("gfb", [128, D], F32, p5)
    fst = sb("fst", [128, 4], F32, p5)
    xn5 = sb("xn5", [128, D], F32, p5)
    junk5 = xn5
    ot = [sb("ot%d" % i, [128, D], F32, p5) for i in range(2)]
    xissued = set()

    def P1a(ti):
        if ti in xissued or ti >= 16:
            return
        xissued.add(ti)
        xi = ti % 2
        dma('sp', xr[xi][:], xw.ap()[WIN - TOK + ti * 128:WIN - TOK + (ti + 1) * 128, :], [], ['xr%d' % xi], 'xr%d' % xi)

    P1a(0)
    P1a(1)
    dma('sp', gfb[:], AP(g_fin, 0, [[0, 128], [1, D]]), [], ['gfb'], 'gfb')
    dma('sp', gffb[:], AP(g_ffn, 0, [[0, 128], [1, D]]), [], ['gffb'], 'gffb')
    wslot = 0
    xcount = 0
    xb5 = sb("xb5", [128, D], BF16, p5)
    identb5 = sb("identb5", [128, 128], BF16, p5)
    R.op('dve', lambda e: e.tensor_copy(out=identb5[:], in_=ident[:]), ['ident'], ['identb5'])
    pt5 = [ps[:, 2 + a_, :].bitcast(BF16) for a_ in range(2)]

    def P1(ti, tl):
        P1a(ti)
        xi = ti % 2
        for half in range(2):
            bk = 2 * tl + half
            for k in range(8):
                mm(bank(bk), mixT[:, k, ti * 128:(ti + 1) * 128], wob[:, k, half * 512:(half + 1) * 512], k == 0, k == 7,
                   ['mixT', 'wob'], ['psb%d' % bk])
        for half in range(2):
            bk = 2 * tl + half
            tt('dve', x1[:, tl, half * 512:(half + 1) * 512], bank(bk), xr[xi][:, half * 512:(half + 1) * 512], ALU.add,
               ['psb%d' % bk, 'xr%d' % xi], ['x1_%d' % tl])
        P1a(ti + 2)

    def P2(ti, tl):
        R.op('dve', lambda e, tl=tl: e.scalar_tensor_tensor(out=xn5[:], in0=x1[:, tl, :], scalar=1.0, in1=x1[:, tl, :],
                                                          op0=ALU.mult, op1=ALU.mult, accum_out=fst[:, 0:1]), ['x1_%d' % tl], ['xn5', 'fst0'])
        act(fst[:, 0:1], fst[:, 0:1], AF.Sqrt, ['fst0', 'epsc'], ['fst0'], scale=1.0 / D, bias=epsc[:])
        R.op('dve', lambda e, o_=fst[:, 0:1]: e.reciprocal(out=o_, in_=o_), ['fst0'], ['fst0'])
        R.op('dve', lambda e, tl=tl: e.scalar_tensor_tensor(out=xb5[:], in0=x1[:, tl, :], scalar=fst[:, 0:1], in1=gffb[:],
                                                          op0=ALU.mult, op1=ALU.mult), ['x1_%d' % tl, 'fst0', 'gffb'], ['xb5'])
        bk = 2 * tl
        ptv = ps[:, bk, :].bitcast(BF16)
        for k in range(8):
            R.op('pe', lambda e, k=k, ptv=ptv: e.transpose(ptv[:, k * 128:(k + 1) * 128], xb5[:, k * 128:(k + 1) * 128], identb5[:]),
                 ['xb5', 'identb5'], ['psb%d' % bk], inc=(k == 7))
        src = ptv.rearrange("p (a b) -> p a b", a=8)
        dst = h2T[:, :, tl * 128:(tl + 1) * 128]
        R.op('act', lambda e, src=src, dst=dst: e.copy(out=dst, in_=src), ['psb%d' % bk], ['h2T'])

    wl_next = [0]

    def wload(upto):
        while wl_next[0] <= min(upto, 43):
            g = wl_next[0]
            fp_ = g % 11
            s_ = g % 3
            dma('sp', wgr[s_][:], AP(wg_s, fp_ * 256, [[DFF, 128], [128 * DFF, 8], [1, 256]]), ['wg_s0', 'wg_s1'], ['wgr%d' % s_], 'wgr%d' % s_)
            dma('sp', wur[s_][:], AP(wu_s, fp_ * 256, [[DFF, 128], [128 * DFF, 8], [1, 256]]), ['wu_s0', 'wu_s1'], ['wur%d' % s_], 'wur%d' % s_)
            wl_next[0] += 1

    wload(1)
    for blk in range(4):
        for tl in range(4):
            P1(blk * 4 + tl, tl)
        for tl in range(4):
            P2(blk * 4 + tl, tl)
        for fp in range(11):
            s = (blk * 11 + fp) % 3
            wload(blk * 11 + fp + 2)
            for fi in range(2):
                f = fp * 2 + fi
                pb_ = 4 + 2 * (f % 2)
                for k in range(8):
                    mm(bank(pb_), wgr[s][:, k, fi * 128:(fi + 1) * 128], h2T[:, k, :], k == 0, k == 7, ['wgr%d' % s, 'h2T'], ['psb%d' % pb_])
                for k in range(8):
                    mm(bank(pb_ + 1), wur[s][:, k, fi * 128:(fi + 1) * 128], h2T[:, k, :], k == 0, k == 7, ['wur%d' % s, 'h2T'],
                       ['psb%d' % (pb_ + 1)])
                act(sl[f % 2][:], bank(pb_), AF.Silu, ['psb%d' % pb_], ['sl%d' % (f % 2)])
                tt('dve', actT[:, f, :], sl[f % 2][:], bank(pb_ + 1), ALU.mult, ['sl%d' % (f % 2), 'psb%d' % (pb_ + 1)], ['actT'])
        P1a((blk + 1) * 4)
        P1a((blk + 1) * 4 + 1)
        for tl in range(4):
            ti = blk * 4 + tl
            oi = ti % 2
            for half in range(2):
                for f in range(NF):
                    mm(bank(half), actT[:, f, tl * 128:(tl + 1) * 128], wdb[:, f, half * 512:(half + 1) * 512], f == 0, f == NF - 1,
                       ['actT', 'wdb'], ['psb%d' % half])
                tt('dve', x1[:, tl, half * 512:(half + 1) * 512], bank(half), x1[:, tl, half * 512:(half + 1) * 512], ALU.add,
                   ['psb%d' % half, 'x1_%d' % tl], ['x1_%d' % tl])
            act(junk5[:], x1[:, tl, :], AF.Square, ['x1_%d' % tl], ['xn5', 'fst1'], accum=fst[:, 1:2])
            ts('dve', fst[:, 1:2], fst[:, 1:2], 1.0 / D, EPS, ALU.mult, ALU.add, ['fst1'], ['fst1'])
            act(fst[:, 1:2], fst[:, 1:2], AF.Sqrt, ['fst1'], ['fst1']); R.op('dve', lambda e, o_=fst[:, 1:2]: e.reciprocal(out=o_, in_=o_), ['fst1'], ['fst1'])
            R.op('dve', lambda e, tl=tl, oi=oi: e.scalar_tensor_tensor(out=ot[oi][:], in0=x1[:, tl, :], scalar=fst[:, 1:2], in1=gfb[:],
                                                                      op0=ALU.mult, op1=ALU.mult), ['x1_%d' % tl, 'fst1', 'gfb'], ['ot%d' % oi])
            dma('pool', out.ap()[ti * 128:(ti + 1) * 128, :], ot[oi][:], ['ot%d' % oi], ['out%d' % oi], 'ot%d' % oi)
    R.op('sp', None, ['out0', 'out1'], [])
    R.emit("p5")
    p5.close()
    st.close()
    return nc


_NC = None


def kernel(**inputs):
    global _NC
    x = np.ascontiguousarray(np.asarray(inputs["x"], dtype=np.float32))
    B, S, _ = x.shape
    shared = {}
    for k, v in inputs.items():
        if k == "x":
            continue
        a = np.ascontiguousarray(np.asarray(v, dtype=np.float32))
        if a.ndim >= 2 and a.shape[0] == 1:
            a = a.reshape(a.shape[1:])
        if k == "ssm_d":
            a = a.reshape(512)
        shared[k] = np.ascontiguousarray(a)
    ident = np.eye(128, dtype=np.float32)
    bdm = np.kron(np.eye(8, dtype=np.float32), np.ones((16, 16), np.float32))
    ql = np.arange(128)[:, None]; kc = np.arange(640)[None, :]
    band_idx = kc // 64 - ql // 64
    band = np.where((band_idx >= 0) & (band_idx <= 8), 0.0, -1e30).astype(np.float32)
    shared["c_ident"] = ident; shared["c_bdm"] = bdm; shared["c_band"] = band
    in_maps = []
    for c in range(8):
        b, q = c // 4, c % 4
        end = (q + 1) * TOK
        xwin = np.zeros((WIN, D), np.float32)
        xwin[WIN - end:] = x[b, :end]
        m = dict(shared)
        m["xw"] = xwin
        m["c_hb"] = np.full((128, 512), -1e30 if q == 0 else 0.0, np.float32)
        in_maps.append(m)
    if _NC is None:
        _NC = build_program()
    res = run_bass_kernel_spmd(_NC, in_maps, core_ids=list(range(8)))
    outp = np.zeros((B, S, D), np.float32)
    for c in range(8):
        b, q = c // 4, c % 4
        outp[b, q * TOK:(q + 1) * TOK] = res.results[c]["out"]
    return outp
```
